# Optimizing a Trainium2 kernel written in Bass

```python
import jax, jax.numpy as jnp
from jax import lax
import numpy as np

D_MODEL = 1024
BATCH = 4
SEQ = 4096
DEPTH = 2

GRID_W = 64
CTX_LEN = 256
N_Q_HEADS = 8
N_KV_HEADS = 2
HEAD_DIM = 64
ATTN_WIDTH = N_Q_HEADS * HEAD_DIM
KV_WIDTH = N_KV_HEADS * HEAD_DIM
Q_BLOCK = 128
ROPE_THETA = 10000.0
POOL_WINDOWS = (2, 4, 8, 16)
N_POOL_GROUPS = 4
POOL_GROUP_DIM = 128
POOL_WIDTH = N_POOL_GROUPS * POOL_GROUP_DIM
MIX_IN_WIDTH = ATTN_WIDTH + 2 * KV_WIDTH + POOL_WIDTH
MIX_OUT_WIDTH = ATTN_WIDTH + POOL_WIDTH
CONV_WIDTH = D_MODEL
CONV_K = 3
N_EXPERTS = 16
EXPERT_FF = 1024
CAPACITY_FACTOR = 2
NORM_EPS = 1e-6
DEEPNORM_ALPHA = (2 * DEPTH) ** 0.25
DEEPNORM_BETA = (8 * DEPTH) ** -0.25

kernel_name = 'hybrid_flow_trunk_gqa_pool_shortconv_ecmoe'


def _layer_norm(x, g, b):
    xf = x.astype(jnp.float32)
    mu = jnp.mean(xf, axis=-1, keepdims=True)
    xc = xf - mu
    var = jnp.mean(xc * xc, axis=-1, keepdims=True)
    y = xc * lax.rsqrt(var + NORM_EPS) * g.astype(jnp.float32) + b.astype(jnp.float32)
    return y.astype(x.dtype)


def _rms_norm(x, g):
    xf = x.astype(jnp.float32)
    y = xf * lax.rsqrt(jnp.mean(xf * xf, axis=-1, keepdims=True) + NORM_EPS)
    return (y * g.astype(jnp.float32)).astype(x.dtype)


def _modulation(cond, w, b):
    m = (cond @ w + b)[:, None, :]
    return jnp.split(m, 6, axis=-1)


def _modulate(h, shift, scale):
    return h * (1.0 + scale) + shift


def _post_norm(h, branch, g, b):
    return _layer_norm(DEEPNORM_ALPHA * h + branch, g, b)


def _axial_rope_tables(rows):
    row = jnp.broadcast_to(jnp.arange(rows, dtype=jnp.float32)[:, None], (rows, GRID_W)).reshape(-1)
    col = jnp.broadcast_to(jnp.arange(GRID_W, dtype=jnp.float32)[None, :], (rows, GRID_W)).reshape(-1)
    axis_dim = HEAD_DIM // 2
    inv_freq = ROPE_THETA ** (-jnp.arange(0, axis_dim, 2, dtype=jnp.float32) / axis_dim)
    ang = jnp.concatenate([row[:, None] * inv_freq, col[:, None] * inv_freq], axis=-1)
    return jnp.cos(ang), jnp.sin(ang)


def _apply_rope(x, cos, sin):
    xf = x.astype(jnp.float32).reshape(x.shape[:-1] + (HEAD_DIM // 2, 2))
    x0, x1 = xf[..., 0], xf[..., 1]
    c = cos[None, :, None, :]
    s = sin[None, :, None, :]
    out = jnp.stack([x0 * c - x1 * s, x0 * s + x1 * c], axis=-1).reshape(x.shape)
    return out.astype(x.dtype)


def _blocked_attention(q, k, v):
    b, lq = q.shape[:2]
    grp = N_Q_HEADS // N_KV_HEADS
    nb = lq // Q_BLOCK
    qb = q.reshape(b, nb, Q_BLOCK, N_KV_HEADS, grp, HEAD_DIM).transpose(1, 0, 2, 3, 4, 5)
    scale = HEAD_DIM ** -0.5

    def one_block(qi):
        s = jnp.einsum('bqhgd,bkhd->bhgqk', qi, k, preferred_element_type=jnp.float32) * scale
        p = jax.nn.softmax(s, axis=-1)
        return jnp.einsum('bhgqk,bkhd->bqhgd', p.astype(v.dtype), v)

    o = lax.map(one_block, qb)
    return o.transpose(1, 0, 2, 3, 4, 5).reshape(b, lq, ATTN_WIDTH)


def _pool_branch(p, w_grp, p_scale):
    b, l, _ = p.shape
    pf = p.astype(jnp.float32)
    cs = jnp.concatenate([jnp.zeros((b, 1, POOL_WIDTH), jnp.float32), jnp.cumsum(pf, axis=1)], axis=1)
    t = jnp.arange(l)
    means = []
    for g, w in enumerate(POOL_WINDOWS):
        lo = jnp.maximum(t - w // 2, 0)
        hi = jnp.minimum(t + w // 2, l)
        seg = cs[:, :, g * POOL_GROUP_DIM:(g + 1) * POOL_GROUP_DIM]
        cnt = (hi - lo).astype(jnp.float32)[None, :, None]
        means.append((seg[:, hi] - seg[:, lo]) / cnt)
    pooled = (jnp.concatenate(means, axis=-1) - pf).astype(p.dtype)
    pooled = pooled.reshape(b, l, N_POOL_GROUPS, POOL_GROUP_DIM)
    out = jnp.einsum('blgc,gcd->blgd', pooled, w_grp).reshape(b, l, POOL_WIDTH)
    return out * p_scale


def _split_mix_in(h):
    return jnp.split(h, [ATTN_WIDTH, ATTN_WIDTH + KV_WIDTH, ATTN_WIDTH + 2 * KV_WIDTH], axis=-1)


def _attn_pool_mixer(u_lat, u_ctx, cos, sin, w_in, q_g, k_g, w_grp, p_scale, w_out, ctx_out):
    b, s, _ = u_lat.shape
    lc = u_ctx.shape[1]
    q, k, v, p = _split_mix_in(u_lat @ w_in)
    q = _apply_rope(_rms_norm(q.reshape(b, s, N_Q_HEADS, HEAD_DIM), q_g), cos, sin)
    k = _apply_rope(_rms_norm(k.reshape(b, s, N_KV_HEADS, HEAD_DIM), k_g), cos, sin)
    v = v.reshape(b, s, N_KV_HEADS, HEAD_DIM)
    if ctx_out:
        qc, kc, vc, pc = _split_mix_in(u_ctx @ w_in)
    else:
        kc, vc = jnp.split(u_ctx @ w_in[:, ATTN_WIDTH:ATTN_WIDTH + 2 * KV_WIDTH], 2, axis=-1)
    kc = _rms_norm(kc.reshape(b, lc, N_KV_HEADS, HEAD_DIM), k_g)
    vc = vc.reshape(b, lc, N_KV_HEADS, HEAD_DIM)
    k_all = jnp.concatenate([kc, k], axis=1)
    v_all = jnp.concatenate([vc, v], axis=1)
    a_lat = _blocked_attention(q, k_all, v_all)
    y_lat = jnp.concatenate([a_lat, _pool_branch(p, w_grp, p_scale)], axis=-1) @ w_out
    y_ctx = None
    if ctx_out:
        qc = _rms_norm(qc.reshape(b, lc, N_Q_HEADS, HEAD_DIM), q_g)
        a_ctx = _blocked_attention(qc, kc, vc)
        y_ctx = jnp.concatenate([a_ctx, _pool_branch(pc, w_grp, p_scale)], axis=-1) @ w_out
    return y_lat, y_ctx


def _short_conv_mixer(u, w_in, conv_w, w_out):
    b_gate, c_gate, x_in = jnp.split(u @ w_in, 3, axis=-1)
    z = lax.conv_general_dilated(c_gate * x_in, conv_w[:, None, :], window_strides=(1,),
                                 padding=((CONV_K // 2, CONV_K // 2),),
                                 dimension_numbers=('NWC', 'WIO', 'NWC'),
                                 feature_group_count=CONV_WIDTH)
    return (b_gate * z) @ w_out


def _expert_choice_ffn(u, w_router, w_g, w_u, w_d):
    b, n, _ = u.shape
    cap = CAPACITY_FACTOR * n // N_EXPERTS
    logits = jnp.einsum('bnd,de->ben', u, w_router, preferred_element_type=jnp.float32)
    aff = jax.nn.softmax(logits, axis=1)
    gate, idx = lax.top_k(aff, cap)
    bidx = jnp.arange(b)[:, None, None]
    xe = u[bidx, idx]
    hg = jnp.einsum('becd,edf->becf', xe, w_g)
    hu = jnp.einsum('becd,edf->becf', xe, w_u)
    ye = jnp.einsum('becf,efd->becd', jax.nn.silu(hg) * hu, w_d) * gate[..., None].astype(u.dtype)
    return jnp.zeros_like(u).at[bidx, idx].add(ye)


def setup_inputs(seed: int = 0) -> dict:
    key = jax.random.key(seed)
    ks = jax.random.split(key, 24)
    f32 = jnp.float32
    n_even = (DEPTH + 1) // 2
    n_odd = DEPTH // 2

    def dense(k, shape, fan_in, mult=1.0):
        return jax.random.normal(k, shape, f32) * (mult * fan_in ** -0.5)

    def gain(k, shape):
        return 1.0 + 0.02 * jax.random.normal(k, shape, f32)

    def small(k, shape):
        return 0.02 * jax.random.normal(k, shape, f32)

    return {
        'x': jax.random.normal(ks[0], (BATCH, SEQ, D_MODEL), f32),
        'c': jax.random.normal(ks[1], (BATCH, D_MODEL), f32),
        'ctx': jax.random.normal(ks[2], (BATCH, CTX_LEN, D_MODEL), f32),
        'c_ctx': jax.random.normal(ks[3], (D_MODEL,), f32),
        'w_mod': dense(ks[4], (DEPTH, D_MODEL, 6 * D_MODEL), D_MODEL, 0.5),
        'b_mod': 0.01 * jax.random.normal(ks[5], (DEPTH, 6 * D_MODEL), f32),
        'ln_mix_g': gain(ks[6], (DEPTH, D_MODEL)),
        'ln_mix_b': small(ks[7], (DEPTH, D_MODEL)),
        'ln_ffn_g': gain(ks[8], (DEPTH, D_MODEL)),
        'ln_ffn_b': small(ks[9], (DEPTH, D_MODEL)),
        'w_mix_in': dense(ks[10], (n_even, D_MODEL, MIX_IN_WIDTH), D_MODEL),
        'q_norm_g': gain(ks[11], (n_even, HEAD_DIM)),
        'k_norm_g': gain(ks[12], (n_even, HEAD_DIM)),
        'w_pool_grp': dense(ks[13], (n_even, N_POOL_GROUPS, POOL_GROUP_DIM, POOL_GROUP_DIM), POOL_GROUP_DIM),
        'pool_scale': gain(ks[14], (n_even, POOL_WIDTH)),
        'w_mix_out': dense(ks[15], (n_even, MIX_OUT_WIDTH, D_MODEL), MIX_OUT_WIDTH, DEEPNORM_BETA),
        'w_conv_in': dense(ks[16], (n_odd, D_MODEL, 3 * CONV_WIDTH), D_MODEL),
        'conv_w': dense(ks[17], (n_odd, CONV_K, CONV_WIDTH), CONV_K),
        'w_conv_out': dense(ks[18], (n_odd, CONV_WIDTH, D_MODEL), CONV_WIDTH, DEEPNORM_BETA),
        'w_router': dense(ks[19], (DEPTH, D_MODEL, N_EXPERTS), D_MODEL),
        'w_exp_gate': dense(ks[20], (DEPTH, N_EXPERTS, D_MODEL, EXPERT_FF), D_MODEL),
        'w_exp_up': dense(ks[21], (DEPTH, N_EXPERTS, D_MODEL, EXPERT_FF), D_MODEL),
        'w_exp_down': dense(ks[22], (DEPTH, N_EXPERTS, EXPERT_FF, D_MODEL), EXPERT_FF, DEEPNORM_BETA),
    }


def reference(x, c, ctx, c_ctx, w_mod, b_mod, ln_mix_g, ln_mix_b, ln_ffn_g, ln_ffn_b,
              w_mix_in, q_norm_g, k_norm_g, w_pool_grp, pool_scale, w_mix_out,
              w_conv_in, conv_w, w_conv_out,
              w_router, w_exp_gate, w_exp_up, w_exp_down):
    rows = x.shape[1] // GRID_W
    cos, sin = _axial_rope_tables(rows)
    cond_lat = jax.nn.silu(c)
    cond_ctx = jax.nn.silu(c_ctx)[None, :]
    h_lat, h_ctx = x, ctx
    for layer in range(DEPTH):
        is_even = layer % 2 == 0
        ctx_out = any(j % 2 == 0 for j in range(layer + 1, DEPTH))
        ctx_in = is_even or ctx_out
        m_lat = _modulation(cond_lat, w_mod[layer], b_mod[layer])
        u_lat = _modulate(h_lat, m_lat[0], m_lat[1])
        u_ctx = None
        if ctx_in:
            m_ctx = _modulation(cond_ctx, w_mod[layer], b_mod[layer])
            u_ctx = _modulate(h_ctx, m_ctx[0], m_ctx[1])
        if is_even:
            e = layer // 2
            y_lat, y_ctx = _attn_pool_mixer(u_lat, u_ctx, cos, sin, w_mix_in[e], q_norm_g[e], k_norm_g[e],
                                            w_pool_grp[e], pool_scale[e], w_mix_out[e], ctx_out)
        else:
            o = layer // 2
            y_lat = _short_conv_mixer(u_lat, w_conv_in[o], conv_w[o], w_conv_out[o])
            y_ctx = _short_conv_mixer(u_ctx, w_conv_in[o], conv_w[o], w_conv_out[o]) if ctx_out else None
        h_lat = _post_norm(h_lat, m_lat[2] * y_lat, ln_mix_g[layer], ln_mix_b[layer])
        f_lat = _expert_choice_ffn(_modulate(h_lat, m_lat[3], m_lat[4]), w_router[layer],
                                   w_exp_gate[layer], w_exp_up[layer], w_exp_down[layer])
        h_lat = _post_norm(h_lat, m_lat[5] * f_lat, ln_ffn_g[layer], ln_ffn_b[layer])
        if ctx_out:
            h_ctx = _post_norm(h_ctx, m_ctx[2] * y_ctx, ln_mix_g[layer], ln_mix_b[layer])
            f_ctx = _expert_choice_ffn(_modulate(h_ctx, m_ctx[3], m_ctx[4]), w_router[layer],
                                       w_exp_gate[layer], w_exp_up[layer], w_exp_down[layer])
            h_ctx = _post_norm(h_ctx, m_ctx[5] * f_ctx, ln_ffn_g[layer], ln_ffn_b[layer])
    return h_lat
```

```python
import numpy as np
import ml_dtypes
from contextlib import ExitStack, contextmanager
import concourse.bass as bass
import concourse.mybir as mybir
from concourse.bass_utils import run_bass_kernel_spmd

F32 = mybir.dt.float32
BF16 = mybir.dt.bfloat16
I32 = mybir.dt.int32
U32 = mybir.dt.uint32
ALU = mybir.AluOpType
AF = mybir.ActivationFunctionType
AX = mybir.AxisListType

NDMA_SEM = 16
D = 1024
S = 4096
LC = 256
NT = S // 128
NKT = (S + LC) // 128
NE = 16
CAP = 512
ALPHA = float(4.0 ** 0.25)
EPS = 1e-6
WINS = (2, 4, 8, 16)


class Prog:
    def __init__(self, nc):
        self.nc = nc
        self.ops = []
        self.last_w = {}
        self.readers = {}

    def add(self, eng, fn, r=(), w=(), dma=False):
        i = len(self.ops)
        deps = set()
        for k in r:
            if k in self.last_w:
                deps.add(self.last_w[k])
        for k in w:
            if k in self.last_w:
                deps.add(self.last_w[k])
            deps.update(self.readers.get(k, ()))
        for k in r:
            self.readers.setdefault(k, []).append(i)
        for k in w:
            self.last_w[k] = i
            self.readers[k] = []
        deps.discard(i)
        self.ops.append(dict(eng=eng, fn=fn, deps=deps, dma=dma))
        return i

    def barrier(self):
        start = getattr(self, '_bar_start', 0)
        deps = set()
        last = {}
        for i in range(start, len(self.ops)):
            o = self.ops[i]
            if o['fn'] is None:
                continue
            if o['dma']:
                deps.add(i)
            else:
                last[o['eng']] = i
        deps.update(last.values())
        for e in ('pe', 'act', 'dve', 'pool', 'sp'):
            self.ops.append(dict(eng=e, fn=None, deps=set(deps), dma=False))
        self._bar_start = len(self.ops)

    def pe(self, fn, r=(), w=()):
        return self.add('pe', fn, r, w)

    def act(self, fn, r=(), w=()):
        return self.add('act', fn, r, w)

    def dve(self, fn, r=(), w=()):
        return self.add('dve', fn, r, w)

    def pool(self, fn, r=(), w=()):
        return self.add('pool', fn, r, w)

    def dma(self, fn, r=(), w=(), q='sp'):
        return self.add(q, fn, r, w, dma=True)

    def emit(self, es, final_keys=()):
        nc = self.nc
        ops = self.ops
        self.add('sp', None, r=list(final_keys), w=())
        n = len(ops)
        has_dep = [False] * n
        for i, o in enumerate(ops):
            for d in o['deps']:
                if ops[d]['eng'] == 'pe' and o['eng'] == 'pe' and not ops[d]['dma'] and not o['dma']:
                    continue
                has_dep[d] = True
        engs = ['pe', 'act', 'dve', 'pool', 'sp']
        esem = {e: es.enter_context(nc.semaphore("s_" + e)) for e in engs}
        dsem = {q: [es.enter_context(nc.semaphore("d%s%d" % (q, j))) for j in range(NDMA_SEM)] for q in ('sp', 'pool')}
        ecount = {e: 0 for e in engs}
        dcount = {q: [0] * NDMA_SEM for q in ('sp', 'pool')}
        dnext = {'sp': 0, 'pool': 0}
        sig = [None] * n
        prevdma = [None] * n
        for i, o in enumerate(ops):
            if o['fn'] is None:
                continue
            if o['dma']:
                q = o['eng']
                j = dnext[q] % NDMA_SEM
                dnext[q] += 1
                if dcount[q][j] > 0:
                    prevdma[i] = (dsem[q][j], dcount[q][j] * 16, ('d', q, j))
                dcount[q][j] += 1
                sig[i] = (dsem[q][j], dcount[q][j] * 16, ('d', q, j))
            elif has_dep[i]:
                e = o['eng']
                ecount[e] += 1
                sig[i] = (esem[e], ecount[e], ('e', e))
        per_eng = {e: [i for i, o in enumerate(ops) if o['eng'] == e] for e in engs}
        self.stats = {e: len(per_eng[e]) for e in engs}

        def run_engine(ename, eobj):
            waited = {}
            for i in per_eng[ename]:
                o = ops[i]
                need = []
                for d in sorted(o['deps']):
                    if ops[d]['eng'] == 'pe' and ename == 'pe' and not ops[d]['dma'] and not o['dma']:
                        continue
                    need.append(sig[d])
                if prevdma[i] is not None:
                    need.append(prevdma[i])
                for s in need:
                    if s is None:
                        continue
                    sem, val, key = s
                    if waited.get(key, 0) < val:
                        eobj.wait_ge(sem, val)
                        waited[key] = val
                if o['fn'] is None:
                    continue
                ins = o['fn'](eobj)
                if sig[i] is not None:
                    sem, val, key = sig[i]
                    ins.then_inc(sem, 16 if o['dma'] else 1)

        with nc.Block() as block:
            @block.tensor
            def _(e):
                run_engine('pe', e)

            @block.scalar
            def _(e):
                run_engine('act', e)

            @block.vector
            def _(e):
                run_engine('dve', e)

            @block.gpsimd
            def _(e):
                run_engine('pool', e)

            @block.sync
            def _(e):
                run_engine('sp', e)


def build(debug=False, stop=None):
    nc = bass.Bass("TRN2", target_bir_lowering=False)
    P = Prog(nc)

    @contextmanager
    def scope():
        with ExitStack() as st_:
            yield st_
        P.barrier()

    def din(name, shape, dt=F32):
        return nc.dram_tensor(name, list(shape), dt, kind="ExternalInput").ap()

    x = din("x", [S, D]); ctx = din("ctx", [LC, D])
    c_col = din("c_col", [128, 8]); cc_col = din("cc_col", [128, 8])
    w_mod = din("w_mod", [2, D, 6 * D]); b_mod = din("b_mod", [2, 6 * D])
    ln_mix_g = din("ln_mix_g", [2, D]); ln_mix_b = din("ln_mix_b", [2, D])
    ln_ffn_g = din("ln_ffn_g", [2, D]); ln_ffn_b = din("ln_ffn_b", [2, D])
    w_mix_in = din("w_mix_in", [D, 1280])
    gq_col = din("gq_col", [128, 1]); gk_col = din("gk_col", [128, 1])
    w_pool_grp = din("w_pool_grp", [4, 128, 128]); pscale_col = din("pscale_col", [128, 4])
    w_mix_out = din("w_mix_out", [D, D])
    w_conv_in = din("w_conv_in", [D, 3 * D]); convw_col = din("convw_col", [128, 8, 3])
    w_conv_out = din("w_conv_out", [D, D])
    w_router = din("w_router", [2, D, NE])
    w_eg = din("w_exp_gate", [2, NE, D, D]); w_eu = din("w_exp_up", [2, NE, D, D]); w_ed = din("w_exp_down", [2, NE, D, D])
    ident_bf_d = din("ident_bf", [128, 128], BF16); ident_f_d = din("ident_f", [128, 128])
    bones_d = din("bones", [128, 128], BF16); perm_d = din("perm", [128, 128], BF16)
    cosT = din("cosT", [128, S]); sinT = din("sinT", [128, S]); invcnt = din("invcnt", [4, S])
    out = nc.dram_tensor("out", [S, D], F32, kind="ExternalOutput").ap()
    kind_s = "ExternalOutput" if debug else "Internal"

    def dscr(name, shape, dt=F32):
        return nc.dram_tensor(name, list(shape), dt, kind=kind_s).ap()

    H = dscr("H", [S, D]); H2 = dscr("H2", [S, D]); U2 = dscr("U2", [S, D], BF16); Fb = dscr("Fb", [S, D])
    PT = dscr("PT", [4, 128, S + 16])
    BZ = dscr("BZ", [128, 8, S], BF16)
    AFD = dscr("AFD", [16, S]); CD = dscr("CD", [128, 128])
    dbg = {}
    if debug:
        dbg["d_mod0"] = nc.dram_tensor("d_mod0", [128, 6 * D], F32, kind="ExternalOutput").ap()
        dbg["d_modc"] = nc.dram_tensor("d_modc", [128, 2 * D], F32, kind="ExternalOutput").ap()
        dbg["d_qt"] = nc.dram_tensor("d_qt", [128, 4 * S], BF16, kind="ExternalOutput").ap()
        dbg["d_kt"] = nc.dram_tensor("d_kt", [128, S + LC], BF16, kind="ExternalOutput").ap()
        dbg["d_va"] = nc.dram_tensor("d_va", [128, NKT * 256], BF16, kind="ExternalOutput").ap()
        dbg["d_afft"] = nc.dram_tensor("d_afft", [2, 16, S], F32, kind="ExternalOutput").ap()
        dbg["d_idx"] = nc.dram_tensor("d_idx", [2, 128, 64], I32, kind="ExternalOutput").ap()
        dbg["d_gate"] = nc.dram_tensor("d_gate", [2, 128, 64], F32, kind="ExternalOutput").ap()
        dbg["d_y"] = nc.dram_tensor("d_y", [2, S, D], F32, kind="ExternalOutput").ap()

    final_keys = ['out%d' % t for t in range(NT)]
    _uid = [0]
    with ExitStack() as es:
        def sb(name, shape, dt, st=es):
            _uid[0] += 1
            t_ = st.enter_context(nc.sbuf_tensor("sb%d_%s" % (_uid[0], name), list(shape), dt))
            P.min_free = min(getattr(P, 'min_free', 1 << 30), nc.sbuf_bytes_remaining)
            return t_

        PS = es.enter_context(nc.psum_tensor("PS", [128, 4096], F32))

        def bank(i, n=1):
            return PS[:, i * 512:(i + n) * 512]

        def pk(i):
            return 'ps%d' % i

        ident_bf = sb("ident_bf", [128, 128], BF16); ident_f = sb("ident_f", [128, 128], F32)
        bones = sb("bones", [128, 128], BF16); perm = sb("perm", [128, 128], BF16)
        gq = sb("gq", [128, 1], F32); gk = sb("gk", [128, 1], F32)
        pscale = sb("pscale", [128, 4], F32); convw = sb("convw", [128, 8, 3], F32)
        ones_f = sb("ones_f", [128, 128], F32)
        affT = sb("affT", [16, S], F32)
        idxT = sb("idxT", [128, 4, NE], I32); gateT = sb("gateT", [128, 4, NE], F32)
        for (t, src, key) in [(ident_bf, ident_bf_d, 'ident_bf'), (ident_f, ident_f_d, 'ident_f'), (bones, bones_d, 'bones'),
                              (perm, perm_d, 'perm'), (gq, gq_col, 'gq'), (gk, gk_col, 'gk'), (pscale, pscale_col, 'pscale'),
                              (convw, convw_col, 'convw')]:
            (lambda t, src, key: P.dma(lambda e: e.dma_start(out=t[:], in_=src), w=[key]))(t, src, key)
        P.dve(lambda e: e.memset(ones_f[:], 1.0), w=['ones_f'])
        epsc = sb("epsc", [128, 1], F32)
        P.dve(lambda e: e.memset(epsc[:], EPS), w=['epsc'])

        def dma_in(dst, src, wkeys, rkeys=(), q='sp'):
            P.dma(lambda e: e.dma_start(out=dst, in_=src), r=list(rkeys), w=list(wkeys), q=q)

        def mm(o, lhsT, rhs, start, stop, r, w):
            P.pe(lambda e: e.matmul(o, lhsT=lhsT, rhs=rhs, start=start, stop=stop), r=r, w=w)

        def tr(o, in_, ident, r, w):
            P.pe(lambda e: e.transpose(o, in_, ident), r=r, w=w)

        def tt(eng, o, a, b, op, r, w):
            P.add(eng, lambda e: e.tensor_tensor(out=o, in0=a, in1=b, op=op), r=r, w=w)

        def ts(eng, o, a, s1, s2, op0, op1, r, w):
            if op1 is None:
                P.add(eng, lambda e: e.tensor_scalar(out=o, in0=a, scalar1=s1, scalar2=None, op0=op0), r=r, w=w)
            else:
                P.add(eng, lambda e: e.tensor_scalar(out=o, in0=a, scalar1=s1, scalar2=s2, op0=op0, op1=op1), r=r, w=w)

        def stt(o, a, sc, b, op0, op1, r, w):
            P.dve(lambda e: e.scalar_tensor_tensor(out=o, in0=a, scalar=sc, in1=b, op0=op0, op1=op1), r=r, w=w)

        def actf(o, a, func, r, w, scale=None, bias=None, accum=None):
            kw = {}
            if scale is not None:
                kw['scale'] = scale
            if bias is not None:
                kw['bias'] = bias
            if accum is not None:
                kw['accum_out'] = accum
            P.act(lambda e: e.activation(out=o, in_=a, func=func, **kw), r=r, w=w)

        def cp(eng, o, a, r, w):
            if eng == 'act':
                P.act(lambda e: e.copy(out=o, in_=a), r=r, w=w)
            else:
                P.add(eng, lambda e: e.tensor_copy(out=o, in_=a), r=r, w=w)

        def phase_mod(layer, cond_col, modbuf, nchunks, tag, st):
            cond = sb("cond" + tag, [128, 8], F32, st)
            condB = sb("condB" + tag, [128, 8, 128], BF16, st)
            dma_in(cond[:], cond_col, ['cond' + tag])
            actf(cond[:], cond[:], AF.Silu, r=['cond' + tag], w=['cond' + tag])
            for k in range(8):
                ts('dve', condB[:, k, :], ones_f[:], cond[:, k:k + 1], None, ALU.mult, None,
                   r=['cond' + tag, 'ones_f'], w=['condB' + tag])
            wm = [sb("wm%d" % s_ + tag, [128, 8, 512], BF16, st) for s_ in range(4)]
            bm = [sb("bm%d" % s_ + tag, [128, 512], F32, st) for s_ in range(4)]
            wv = w_mod[layer].rearrange("(k p) n -> p k n", p=128)
            for n in range(nchunks):
                s_ = n % 4
                j, half = n // 2, n % 2
                dma_in(wm[s_][:], wv[:, :, n * 512:(n + 1) * 512], ['wm%d' % s_ + tag], q='pool')
                dma_in(bm[s_][:], b_mod[layer:layer + 1, n * 512:(n + 1) * 512].broadcast_to([128, 512]), ['bm%d' % s_ + tag])
                b_ = 6 + (s_ % 2)
                for k in range(8):
                    mm(bank(b_), condB[:, k, :], wm[s_][:, k, :], k == 0, k == 7,
                       r=['condB' + tag, 'wm%d' % s_ + tag], w=[pk(b_)])
                o = modbuf[:, j, half * 512:(half + 1) * 512]
                if j in (1, 4):
                    stt(o, bank(b_), 1.0, bm[s_][:], ALU.add, ALU.add, r=[pk(b_), 'bm%d' % s_ + tag], w=[tag])
                else:
                    tt('dve', o, bank(b_), bm[s_][:], ALU.add, r=[pk(b_), 'bm%d' % s_ + tag], w=[tag])

        def load_ln_rows(layer, st, tag, which):
            rows = {}
            for nm, src in ((('mg', ln_mix_g), ('mb', ln_mix_b)) if which == 'm' else (('fg', ln_ffn_g), ('fb', ln_ffn_b))):
                t = sb("ln_" + nm + tag, [128, D], F32, st)
                dma_in(t[:], src[layer:layer + 1, :].broadcast_to([128, D]), ['ln_' + nm + tag])
                rows[nm] = (t, 'ln_' + nm + tag)
            return rows

        def lockstep(gens):
            gens = list(gens)
            while gens:
                for g_ in list(gens):
                    try:
                        next(g_)
                    except StopIteration:
                        gens.remove(g_)

        def layer_norm_g(t2, k_t2, grow, brow, stats, k_st, o, k_o, aff_eng='dve'):
            for hh in range(2):
                (lambda hh: P.dve(lambda e: e.bn_stats(out=stats[:, hh * 6:(hh + 1) * 6], in_=t2[:, hh * 512:(hh + 1) * 512]),
                                  r=[k_t2], w=[k_st]))(hh)
            P.dve(lambda e: e.bn_aggr(out=stats[:, 12:14], in_=stats[:, 0:12]), r=[k_st], w=[k_st])
            yield
            actf(stats[:, 14:15], stats[:, 13:14], AF.Sqrt, r=[k_st, 'epsc'], w=[k_st], scale=1.0, bias=epsc[:, 0:1])
            yield
            P.dve(lambda e: e.reciprocal(out=stats[:, 14:15], in_=stats[:, 14:15]), r=[k_st], w=[k_st])
            ts('dve', stats[:, 15:16], stats[:, 12:13], stats[:, 14:15], -1.0, ALU.mult, ALU.mult, r=[k_st], w=[k_st])
            yield
            actf(t2, t2, AF.Identity, r=[k_t2, k_st], w=[k_t2], scale=stats[:, 14:15], bias=stats[:, 15:16])
            yield
            tt(aff_eng, t2, t2, grow[0][:], ALU.mult, r=[k_t2, grow[1]], w=[k_t2])
            tt(aff_eng, o, t2, brow[0][:], ALU.add, r=[k_t2, brow[1]], w=[k_o])
            yield

        def mixer_post_g(ti, res_src, y_emit, modbuf, k_mod, ln, W, wr_sb, yb, rb):
            L = W['lane']
            t1, stats, h1, u2f, u2b, u2T, sm = W['t1'], W['stats'], W['h1'], W['u2f'], W['u2b'], W['u2T'], W['sm']
            k_t1, k_st, k_h1, k_u2f, k_u2b, k_u2T, k_sm = ['pw_%s%d' % (n_, L) for n_ in ('t1', 'st', 'h1', 'u2f', 'u2b', 'u2T', 'sm')]
            dma_in(t1[:], res_src[ti * 128:(ti + 1) * 128, :], [k_t1], W.get('res_keys', lambda t: [])(ti))
            y_emit(yb)
            yield
            stt(t1[:], t1[:], ALPHA, bank(yb, 2), ALU.mult, ALU.add, r=[k_t1, pk(yb), pk(yb + 1)], w=[k_t1])
            yield
            for _ in layer_norm_g(t1[:], k_t1, ln['mg'], ln['mb'], stats, k_st, h1[:], k_h1, 'pool'):
                yield
            dma_in(H[ti * 128:(ti + 1) * 128, :], h1[:], ['H%d' % ti], [k_h1], q='pool')
            tt('dve', u2f[:], h1[:], modbuf[:, 4, :], ALU.mult, r=[k_h1, k_mod], w=[k_u2f])
            tt('dve', u2f[:], u2f[:], modbuf[:, 3, :], ALU.add, r=[k_u2f, k_mod], w=[k_u2f])
            yield
            cp('act', u2b[:], u2f[:], r=[k_u2f], w=[k_u2b])
            dma_in(U2[ti * 128:(ti + 1) * 128, :], u2b[:], ['U2_%d' % ti], [k_u2b], q='pool')
            for half in range(2):
                b_ = rb + half
                for kk in range(4):
                    k = half * 4 + kk
                    tr(bank(b_)[:, kk * 128:(kk + 1) * 128], u2f[:, k * 128:(k + 1) * 128], ident_f[:],
                       r=[k_u2f, 'ident_f'], w=[pk(b_)])
            yield
            for half in range(2):
                b_ = rb + half
                cp('act' if half == 0 else 'dve', u2T[:, half * 4:(half + 1) * 4, :],
                   bank(b_).rearrange("p (k t) -> p k t", k=4), r=[pk(b_)], w=[k_u2T])
            yield
            lg = bank(rb)[:, 0:NE]
            for k in range(8):
                mm(lg, u2T[:, k, :], wr_sb[:, k, :], k == 0, k == 7, r=[k_u2T, 'wr'], w=[pk(rb)])
            yield
            P.dve(lambda e: e.tensor_reduce(out=sm[:, 16:17], in_=lg, axis=AX.X, op=ALU.max), r=[pk(rb)], w=[k_sm])
            ts('dve', sm[:, 17:18], sm[:, 16:17], -1.0, None, ALU.mult, None, r=[k_sm], w=[k_sm])
            yield
            actf(sm[:, 0:16], lg, AF.Exp, r=[pk(rb), k_sm], w=[k_sm], bias=sm[:, 17:18], scale=1.0)
            yield
            P.dve(lambda e: e.tensor_reduce(out=sm[:, 18:19], in_=sm[:, 0:16], axis=AX.X, op=ALU.add), r=[k_sm], w=[k_sm])
            P.dve(lambda e: e.reciprocal(out=sm[:, 19:20], in_=sm[:, 18:19]), r=[k_sm], w=[k_sm])
            ts('dve', sm[:, 0:16], sm[:, 0:16], sm[:, 19:20], None, ALU.mult, None, r=[k_sm], w=[k_sm])
            yield
            tr(bank(rb + 1)[0:16, 0:128], sm[:, 0:16], ident_f[:], r=[k_sm, 'ident_f'], w=[pk(rb + 1)])
            yield
            cp('dve', affT[:, ti * 128:(ti + 1) * 128], bank(rb + 1)[0:16, 0:128], r=[pk(rb + 1)], w=['affT'])
            yield

        def alloc_post_work(st, nl=2):
            lanes = []
            for L in range(nl):
                W = dict(lane=L)
                W['t1'] = sb("pw_t1_%d" % L, [128, D], F32, st)
                W['stats'] = sb("pw_st_%d" % L, [128, 16], F32, st)
                W['h1'] = sb("pw_h1_%d" % L, [128, D], F32, st)
                W['u2f'] = sb("pw_u2f_%d" % L, [128, D], F32, st)
                W['u2b'] = sb("pw_u2b_%d" % L, [128, D], BF16, st)
                W['u2T'] = sb("pw_u2T_%d" % L, [128, 8, 128], F32, st)
                W['sm'] = sb("pw_sm_%d" % L, [128, 20], F32, st)
                lanes.append(W)
            return lanes

        def phase_route(layer, st):
            vals = sb("rt_vals", [16, CAP], F32, st)
            idxu = sb("rt_idxu", [16, CAP], U32, st)
            idxf = sb("rt_idxf", [16, CAP], F32, st)
            cand = sb("cand", [16, 1024], F32, st)
            a128 = sb("rt_a128", [128, 512], F32, st)
            c128 = sb("rt_c128", [128, 128], F32, st)
            if debug:
                dma_in(dbg["d_afft"][layer], affT[:], ['d_afft'], ['affT'], q='pool')
            dma_in(AFD, affT[:], ['AFD'], ['affT'])
            dma_in(a128[:], AFD.rearrange("e (c j) -> (e c) j", j=512), ['rt_a128'], ['AFD'])
            for r_ in range(16):
                sl = slice(r_ * 8, (r_ + 1) * 8)
                (lambda sl: P.dve(lambda e: e.max(out=c128[:, sl], in_=a128[:]), r=['rt_a128'], w=['rt_c128']))(sl)
                (lambda sl: P.dve(lambda e: e.match_replace(out=a128[:], in_to_replace=c128[:, sl], in_values=a128[:], imm_value=-1.0),
                                  r=['rt_a128', 'rt_c128'], w=['rt_a128']))(sl)
            dma_in(CD, c128[:], ['CD'], ['rt_c128'])
            dma_in(cand[:], CD.rearrange("(e c) r -> e (c r)", c=8), ['cand'], ['CD'])
            for r_ in range(CAP // 8):
                sl = slice(r_ * 8, (r_ + 1) * 8)
                (lambda sl: P.dve(lambda e: e.max(out=vals[:, sl], in_=cand[:]), r=['cand'], w=['rt_vals']))(sl)
                (lambda sl: P.dve(lambda e: e.match_replace(out=cand[:], in_to_replace=vals[:, sl], in_values=cand[:], imm_value=-1.0),
                                  r=['cand', 'rt_vals'], w=['cand']))(sl)
                (lambda sl: P.dve(lambda e: e.max_index(out=idxu[:, sl], in_max=vals[:, sl], in_values=affT[:]),
                                  r=['affT', 'rt_vals'], w=['rt_idxu']))(sl)
            cp('dve', idxf[:], idxu[:], r=['rt_idxu'], w=['rt_idxf'])
            for stl in range(4):
                tr(bank(4)[:, stl * 16:(stl + 1) * 16], idxf[:, stl * 128:(stl + 1) * 128], ident_f[0:16, 0:16],
                   r=['rt_idxf', 'ident_f'], w=[pk(4)])
                tr(bank(5)[:, stl * 16:(stl + 1) * 16], vals[:, stl * 128:(stl + 1) * 128], ident_f[0:16, 0:16],
                   r=['rt_vals', 'ident_f'], w=[pk(5)])
            cp('dve', idxT[:].rearrange("p a b -> p (a b)"), bank(4)[:, 0:64], r=[pk(4)], w=['idxT'])
            cp('dve', gateT[:].rearrange("p a b -> p (a b)"), bank(5)[:, 0:64], r=[pk(5)], w=['gateT'])
            if debug:
                dma_in(dbg["d_idx"][layer], idxT[:].rearrange("p a b -> p (a b)"), ['d_idx'], ['idxT'], q='pool')
                dma_in(dbg["d_gate"][layer], gateT[:].rearrange("p a b -> p (a b)"), ['d_gate'], ['gateT'], q='pool')

        def phase_experts(layer, st):
            with scope() as stz:
                zt = sb("ex_zero", [128, D], F32, stz)
                P.pool(lambda e: e.memset(zt[:], 0.0), w=['ex_zero'])
                for ti in range(NT):
                    dma_in(Fb[ti * 128:(ti + 1) * 128, :], zt[:], ['Fz%d' % ti], ['ex_zero', 'Frd%d' % ti])
            wg = [sb("ex_wg%d" % i, [128, 8, D], BF16, st) for i in range(2)]
            wu = [sb("ex_wu%d" % i, [128, 8, D], BF16, st) for i in range(2)]
            wd = [sb("ex_wd%d" % i, [128, 8, D], BF16, st) for i in range(2)]
            xe = [sb("ex_xe%d" % i, [128, 4, D], BF16, st) for i in range(2)]
            xeT = [sb("ex_xeT%d" % i, [128, 8, CAP], BF16, st) for i in range(2)]
            actT = sb("ex_actT", [128, 8, CAP], BF16, st)
            sg = [sb("ex_sg%d" % i, [128, CAP], F32, st) for i in range(2)]
            yo = [sb("ex_yo%d" % i, [128, D], F32, st) for i in range(2)]
            u2_all_keys = ['U2_%d' % t for t in range(NT)]

            def load_w(e_, which):
                s_ = e_ % 2
                for (buf, src, nm) in ((wg, w_eg, 'wg'), (wu, w_eu, 'wu'), (wd, w_ed, 'wd')):
                    if nm in which:
                        dma_in(buf[s_][:], src[layer, e_].rearrange("(k p) f -> p k f", p=128), ['ex_%s%d' % (nm, s_)], q='pool')

            def gather(e_):
                s_ = e_ % 2
                for stl in range(4):
                    (lambda stl: P.dma(lambda e: e.indirect_dma_start(
                        out=xe[s_][:, stl, :], out_offset=None, in_=U2,
                        in_offset=bass.IndirectOffsetOnAxis(ap=idxT[:, stl, e_:e_ + 1], axis=0)),
                        r=['idxT'] + u2_all_keys, w=['ex_xe%d_%d' % (s_, stl)], q='pool'))(stl)

            def do_T(e_):
                s_ = e_ % 2
                for stl in range(4):
                    b_ = stl % 2
                    pT = bank(b_).bitcast(BF16)
                    for k in range(8):
                        tr(pT[:, k * 128:(k + 1) * 128], xe[s_][:, stl, k * 128:(k + 1) * 128], ident_bf[:],
                           r=['ex_xe%d_%d' % (s_, stl), 'ident_bf'], w=[pk(b_)])
                    cp('act' if stl % 2 == 0 else 'dve', xeT[s_][:, :, stl * 128:(stl + 1) * 128],
                       pT.rearrange("p (k t) -> p k t", k=8), r=[pk(b_)], w=['ex_xeT%d' % s_])

            def do_HGU(e_):
                s_ = e_ % 2
                for fc in range(8):
                    bg = 2 + (fc % 2) * 2
                    bu = bg + 1
                    for k in range(8):
                        mm(bank(bg), wg[s_][:, k, fc * 128:(fc + 1) * 128], xeT[s_][:, k, :], k == 0, k == 7,
                           r=['ex_wg%d' % s_, 'ex_xeT%d' % s_], w=[pk(bg)])
                    for k in range(8):
                        mm(bank(bu), wu[s_][:, k, fc * 128:(fc + 1) * 128], xeT[s_][:, k, :], k == 0, k == 7,
                           r=['ex_wu%d' % s_, 'ex_xeT%d' % s_], w=[pk(bu)])
                    actf(sg[fc % 2][:], bank(bg), AF.Silu, r=[pk(bg)], w=['ex_sg%d' % (fc % 2)])
                    tt('dve', actT[:, fc, :], sg[fc % 2][:], bank(bu), ALU.mult, r=['ex_sg%d' % (fc % 2), pk(bu)], w=['ex_actT'])

            def do_YE(e_):
                s_ = e_ % 2
                for stl in range(4):
                    yb = 6
                    for n in range(2):
                        for fc in range(8):
                            mm(bank(yb + n), actT[:, fc, stl * 128:(stl + 1) * 128], wd[s_][:, fc, n * 512:(n + 1) * 512],
                               fc == 0, fc == 7, r=['ex_actT', 'ex_wd%d' % s_], w=[pk(yb + n)])
                    y2 = bank(yb, 2)
                    if stl % 2 == 0:
                        actf(yo[stl % 2][:], y2, AF.Copy, r=[pk(yb), pk(yb + 1), 'gateT'], w=['ex_yo%d' % (stl % 2)],
                             scale=gateT[:, stl, e_:e_ + 1])
                    else:
                        ts('dve', yo[stl % 2][:], y2, gateT[:, stl, e_:e_ + 1], None, ALU.mult, None,
                           r=[pk(yb), pk(yb + 1), 'gateT'], w=['ex_yo%d' % (stl % 2)])
                    (lambda stl, e_: P.dma(lambda e: e.indirect_dma_start(
                        out=Fb, out_offset=bass.IndirectOffsetOnAxis(ap=idxT[:, stl, e_:e_ + 1], axis=0),
                        in_=yo[stl % 2][:, :], in_offset=None, compute_op=ALU.add),
                        r=['idxT', 'ex_yo%d' % (stl % 2)] + ['Fz%d' % t for t in range(NT)], w=['F'], q='pool'))(stl, e_)

            for e0 in range(2):
                load_w(e0, ('wg', 'wu', 'wd'))
                gather(e0)
            do_T(0)
            for e_ in range(NE):
                if e_ >= 1 and e_ + 1 < NE:
                    gather(e_ + 1)
                do_HGU(e_)
                if e_ + 2 < NE:
                    load_w(e_ + 2, ('wg', 'wu'))
                if e_ + 1 < NE:
                    do_T(e_ + 1)
                do_YE(e_)
                if e_ + 2 < NE:
                    load_w(e_ + 2, ('wd',))

        def phase_ffn_post(layer, src_h, modbuf, k_mod, ln, dst, dst_key, st):
            ht = [sb("fp_h%d" % i, [128, D], F32, st) for i in range(4)]
            ft = [sb("fp_f%d" % i, [128, D], F32, st) for i in range(4)]
            ot = [sb("fp_o%d" % i, [128, D], F32, st) for i in range(4)]
            stt_ = [sb("fp_s%d" % i, [128, 16], F32, st) for i in range(4)]

            def gen(ti, s_):
                dma_in(ht[s_][:], src_h[ti * 128:(ti + 1) * 128, :], ['fp_h%d' % s_], ['H%d' % ti])
                dma_in(ft[s_][:], Fb[ti * 128:(ti + 1) * 128, :], ['fp_f%d' % s_, 'Frd%d' % ti], ['F'])
                yield
                tt('dve', ft[s_][:], ft[s_][:], modbuf[:, 5, :], ALU.mult, r=['fp_f%d' % s_, k_mod], w=['fp_f%d' % s_])
                stt(ft[s_][:], ht[s_][:], ALPHA, ft[s_][:], ALU.mult, ALU.add, r=['fp_h%d' % s_, 'fp_f%d' % s_], w=['fp_f%d' % s_])
                yield
                for _ in layer_norm_g(ft[s_][:], 'fp_f%d' % s_, ln['fg'], ln['fb'], stt_[s_], 'fp_s%d' % s_, ot[s_][:], 'fp_o%d' % s_, 'pool'):
                    yield
                dma_in(dst[ti * 128:(ti + 1) * 128, :], ot[s_][:], [dst_key + '%d' % ti], ['fp_o%d' % s_], q='pool')
                yield

            for tp in range(NT // 4):
                lockstep([gen(tp * 4 + L, L) for L in range(4)])

        with scope() as L0:
            mod0 = sb("mod0", [128, 6, D], F32, L0)
            with scope() as st:
                phase_mod(0, c_col, mod0, 12, 'mod0', st)
            if debug:
                dma_in(dbg["d_mod0"], mod0[:].rearrange("p a b -> p (a b)"), ['d_mod0'], ['mod0'])
                final_keys += ['d_mod0', 'd_modc']
            wr0 = sb("wr0", [128, 8, NE], F32, L0)
            dma_in(wr0[:], w_router[0].rearrange("(k p) e -> p k e", p=128), ['wr'])

            with scope() as A:
                QT = sb("QT", [128, 4, S], BF16, A)
                KT = sb("KT", [128, S + LC], BF16, A)
                VA = sb("VA", [128, NKT, 2, 128], BF16, A)
                P.dve(lambda e: e.memset(VA[:].rearrange("p a b c -> p (a b c)"), 1.0), w=['VA'])
                with scope() as st:
                    zp = sb("zp", [128, 8], F32, st)
                    P.dve(lambda e: e.memset(zp[:], 0.0), w=['zp'])
                    for g_ in range(4):
                        dma_in(PT[g_, :, 0:8], zp[:], ['PT'], ['zp'])
                        dma_in(PT[g_, :, S + 8:S + 16], zp[:], ['PT'], ['zp'])

                with scope() as st:
                    modc = sb("modc", [128, 2, D], F32, st)
                    with scope() as st2:
                        phase_mod(0, cc_col, modc, 4, 'modc', st2)
                    if debug:
                        dma_in(dbg["d_modc"], modc[:].rearrange("p a b -> p (a b)"), ['d_modc'], ['modc'])
                    w_in = sb("w_in", [128, 8, 1280], BF16, st)
                    dma_in(w_in[:], w_mix_in.rearrange("(k p) n -> p k n", p=128), ['w_in'], q='pool')
                    xt = [sb("a1_x%d" % i, [128, D], F32, st) for i in range(2)]
                    ub = [sb("a1_ub%d" % i, [128, D], BF16, st) for i in range(2)]
                    uT = [sb("a1_uT%d" % i, [128, 8, 512], BF16, st) for i in range(2)]
                    cs = [sb("a1_cos%d" % i, [128, 512], F32, st) for i in range(2)]
                    sn = [sb("a1_sin%d" % i, [128, 512], F32, st) for i in range(2)]
                    lanes_t = [(sb("a1_sq%d" % L, [128, 512], BF16, st), sb("a1_rstd%d" % L, [128, 512], F32, st),
                                sb("a1_qn%d" % L, [128, 512], BF16, st), sb("a1_t1%d" % L, [128, 512], F32, st),
                                sb("a1_t2%d" % L, [128, 512], F32, st)) for L in range(2)]
                    pst = [sb("a1_pst%d" % i, [128, 512], F32, st) for i in range(4)]
                    tile_ctr = [0]

                    def make_uT(src, row0, ntiles, modb, k_modb, cj):
                        s2 = cj % 2
                        for t_ in range(ntiles):
                            s_ = tile_ctr[0] % 2
                            tile_ctr[0] += 1
                            dma_in(xt[s_][:], src[row0 + t_ * 128: row0 + (t_ + 1) * 128, :], ['a1_x%d' % s_])
                            tt('pool', xt[s_][:], xt[s_][:], modb[:, 1, :], ALU.mult, r=['a1_x%d' % s_, k_modb], w=['a1_x%d' % s_])
                            tt('pool', ub[s_][:], xt[s_][:], modb[:, 0, :], ALU.add, r=['a1_x%d' % s_, k_modb], w=['a1_ub%d' % s_])
                            b_ = s_
                            pT = bank(b_).bitcast(BF16)
                            for k in range(8):
                                tr(pT[:, k * 128:(k + 1) * 128], ub[s_][:, k * 128:(k + 1) * 128], ident_bf[:],
                                   r=['a1_ub%d' % s_, 'ident_bf'], w=[pk(b_)])
                            cp('act', uT[s2][:, :, t_ * 128:(t_ + 1) * 128], pT.rearrange("p (k t) -> p k t", k=8),
                               r=[pk(b_)], w=['a1_uT%d' % s2])

                    def proj_norm(cj, ncols, col0, gcol, k_g, rope, dst, k_dst, lane):
                        s2 = cj % 2
                        b_, ba = (2, 3) if lane == 0 else (6, 7)
                        sq_, rstd_, qn_, t1_, t2_ = lanes_t[lane]
                        ksq, krs, kqn, kt1, kt2 = ['a1_%s%d' % (n_, lane) for n_ in ('sq', 'rstd', 'qn', 't1', 't2')]
                        for k in range(8):
                            mm(bank(b_)[:, 0:ncols], w_in[:, k, col0:col0 + 128], uT[s2][:, k, 0:ncols], k == 0, k == 7,
                               r=['w_in', 'a1_uT%d' % s2], w=[pk(b_)])
                        yield
                        actf(sq_[:, 0:ncols], bank(b_)[:, 0:ncols], AF.Square, r=[pk(b_)], w=[ksq])
                        yield
                        mm(bank(ba)[:, 0:ncols], bones[:], sq_[:, 0:ncols], True, True, r=['bones', ksq], w=[pk(ba)])
                        yield
                        actf(rstd_[:, 0:ncols], bank(ba)[:, 0:ncols], AF.Sqrt, r=[pk(ba), 'epsc'], w=[krs], scale=1.0 / 64, bias=epsc[:, 0:1])
                        yield
                        (lambda ncols: P.dve(lambda e: e.reciprocal(out=rstd_[:, 0:ncols], in_=rstd_[:, 0:ncols]), r=[krs], w=[krs]))(ncols)
                        yield
                        if rope:
                            stt(qn_[:, 0:ncols], bank(b_)[:, 0:ncols], gcol[:, 0:1], rstd_[:, 0:ncols], ALU.mult, ALU.mult,
                                r=[pk(b_), k_g, krs], w=[kqn])
                            yield
                            mm(bank(ba)[:, 0:ncols], perm[:], qn_[:, 0:ncols], True, True, r=['perm', kqn], w=[pk(ba)])
                            tt('dve', t1_[:, 0:ncols], qn_[:, 0:ncols], cs[s2][:, 0:ncols], ALU.mult, r=[kqn, 'a1_cos%d' % s2], w=[kt1])
                            yield
                            tt('dve', t2_[:, 0:ncols], bank(ba)[:, 0:ncols], sn[s2][:, 0:ncols], ALU.mult, r=[pk(ba), 'a1_sin%d' % s2], w=[kt2])
                            tt('dve', dst, t1_[:, 0:ncols], t2_[:, 0:ncols], ALU.add, r=[kt1, kt2], w=[k_dst])
                            yield
                        else:
                            stt(dst, bank(b_)[:, 0:ncols], gcol[:, 0:1], rstd_[:, 0:ncols], ALU.mult, ALU.mult,
                                r=[pk(b_), k_g, krs], w=[k_dst])
                            yield

                    def proj_v(cj, ntiles, kt0):
                        s2 = cj % 2
                        for t_ in range(ntiles):
                            b_ = 4 + (t_ % 2)
                            for k in range(8):
                                mm(bank(b_)[:, 0:128], uT[s2][:, k, t_ * 128:(t_ + 1) * 128], w_in[:, k, 640:768], k == 0, k == 7,
                                   r=['w_in', 'a1_uT%d' % s2], w=[pk(b_)])
                            cp('act', VA[:, kt0 + t_, :, 0:64], bank(b_)[:, 0:128].rearrange("p (g d) -> p g d", g=2),
                               r=[pk(b_)], w=['VA'])

                    make_uT(ctx, 0, 2, modc, 'modc', 8)
                    make_uT(x, 0, 4, mod0, 'mod0', 1)
                    lockstep([proj_norm(8, 256, 512, gk, 'gk', False, KT[:, 0:256], 'KT', 0)])
                    proj_v(8, 2, 0)
                    for cj in range(8):
                        s2 = (cj + 1) % 2
                        if cj + 1 < 8:
                            make_uT(x, (cj + 1) * 512, 4, mod0, 'mod0', cj + 2)
                        dma_in(cs[s2][:], cosT[:, cj * 512:(cj + 1) * 512], ['a1_cos%d' % s2])
                        dma_in(sn[s2][:], sinT[:, cj * 512:(cj + 1) * 512], ['a1_sin%d' % s2])
                        cjp = cj + 1
                        for qp in range(2):
                            lockstep([proj_norm(cjp, 512, (qp * 2 + L) * 128, gq, 'gq', True,
                                                QT[:, qp * 2 + L, cj * 512:(cj + 1) * 512], 'QT', L) for L in range(2)])
                        lockstep([proj_norm(cjp, 512, 512, gk, 'gk', True, KT[:, LC + cj * 512: LC + (cj + 1) * 512], 'KT', 0)])
                        proj_v(cjp, 4, 2 + cj * 4)
                        for g_ in range(4):
                            b_ = (2, 3, 6, 7)[g_]
                            for k in range(8):
                                mm(bank(b_), w_in[:, k, 768 + g_ * 128: 768 + (g_ + 1) * 128], uT[s2][:, k, :], k == 0, k == 7,
                                   r=['w_in', 'a1_uT%d' % s2], w=[pk(b_)])
                            cp('act', pst[g_][:], bank(b_), r=[pk(b_)], w=['a1_pst%d' % g_])
                            dma_in(PT[g_, :, 8 + cj * 512: 8 + (cj + 1) * 512], pst[g_][:], ['PT_%d_%d' % (g_, cj)], ['a1_pst%d' % g_])
                if debug:
                    dma_in(dbg["d_qt"], QT[:].rearrange("p a b -> p (a b)"), ['d_qt'], ['QT'])
                    dma_in(dbg["d_kt"], KT[:], ['d_kt'], ['KT'])
                    dma_in(dbg["d_va"], VA[:].rearrange("p a b c -> p (a b c)"), ['d_va'], ['VA'])
                    final_keys += ['d_qt', 'd_kt', 'd_va', 'PT']

                if stop != 'A1':
                    with scope() as st:
                        w_oa = sb("w_oa", [64, 8, D], BF16, st)
                        w_op = sb("w_op", [128, 4, D], BF16, st)
                        w_gr = sb("w_gr", [128, 4, 128], BF16, st)
                        dma_in(w_oa[:], w_mix_out[0:512, :].rearrange("(h p) n -> p h n", p=64), ['w_oa'], q='pool')
                        dma_in(w_op[:], w_mix_out[512:1024, :].rearrange("(g p) n -> p g n", p=128), ['w_op'], q='pool')
                        dma_in(w_gr[:], w_pool_grp.rearrange("g c d -> c g d"), ['w_gr'], q='pool')
                        for hh in range(8):
                            tt('pool', w_oa[:, hh, :], w_oa[:, hh, :], mod0[0:64, 2, :], ALU.mult, r=['w_oa', 'mod0'], w=['w_oa'])
                        for g_ in range(4):
                            tt('pool', w_op[:, g_, :], w_op[:, g_, :], mod0[:, 2, :], ALU.mult, r=['w_op', 'mod0'], w=['w_op'])
                        pe_ = [sb("a2_pe%d" % i, [128, 1024], BF16, st) for i in range(3)]
                        rec = sb("a2_rec", [128, 512], F32, st)
                        aT = sb("a2_aT", [64, 8, 512], BF16, st)
                        plT = sb("a2_plT", [128, 4, 512], BF16, st)
                        ptl = [sb("a2_pt%d" % i, [128, 528], F32, st) for i in range(2)]
                        pa = sb("a2_pa", [128, 528], F32, st)
                        pb = sb("a2_pb", [128, 528], F32, st)
                        ic = sb("a2_ic", [128, 512], F32, st)
                        pld4 = [sb("a2_pld%d" % i, [128, 512], BF16, st) for i in range(4)]
                        wk = alloc_post_work(st)
                        ln0 = load_ln_rows(0, st, '0', 'm')
                        for cj in range(8):
                            q0 = cj * 512
                            for g_ in range(4):
                                w_ = WINS[g_]
                                s_ = g_ % 2
                                pt = ptl[s_]
                                dma_in(pt[:], PT[g_, :, q0:q0 + 528], ['a2_pt%d' % s_], ['PT'])
                                dma_in(ic[:], invcnt[g_:g_ + 1, q0:q0 + 512].broadcast_to([128, 512]), ['a2_ic'])
                                cur, kcur, ln_ = pt, 'a2_pt%d' % s_, 528
                                m_ = 1
                                bufs = [(pa, 'a2_pa'), (pb, 'a2_pb')]
                                bi = 0
                                while m_ < w_:
                                    nb, knb = bufs[bi]
                                    bi ^= 1
                                    nl = ln_ - m_
                                    tt('dve', nb[:, 0:nl], cur[:, 0:nl], cur[:, m_:m_ + nl], ALU.add, r=[kcur], w=[knb])
                                    cur, kcur, ln_ = nb, knb, nl
                                    m_ *= 2
                                o0 = 8 - w_ // 2
                                nb, knb = bufs[bi]
                                tt('dve', nb[:, 0:512], cur[:, o0:o0 + 512], ic[:], ALU.mult, r=[kcur, 'a2_ic'], w=[knb])
                                tt('dve', pld4[g_][:], nb[:, 0:512], pt[:, 8:520], ALU.subtract, r=[knb, 'a2_pt%d' % s_], w=['a2_pld%d' % g_])
                            LAG = 2
                            SB0 = (0, 2, 6)
                            for qc in range(4):
                                for step in range(NKT + LAG):
                                    if step < NKT:
                                        kt = step
                                        sl_ = kt % 3
                                        b0 = SB0[sl_]
                                        for hb in range(2):
                                            pr = slice(hb * 64, (hb + 1) * 64)
                                            mm(bank(b0 + hb), KT[pr, kt * 128:(kt + 1) * 128], QT[pr, qc, q0:q0 + 512], True, True,
                                               r=['KT', 'QT'], w=[pk(b0 + hb)])
                                        actf(pe_[sl_][:], bank(b0, 2), AF.Exp, r=[pk(b0), pk(b0 + 1)], w=['a2_pe%d' % sl_], scale=0.125)
                                    if step >= LAG:
                                        kt = step - LAG
                                        sl_ = kt % 3
                                        for hb in range(2):
                                            mm(bank(4 + hb), VA[:, kt, hb, :], pe_[sl_][:, hb * 512:(hb + 1) * 512], kt == 0, kt == NKT - 1,
                                               r=['VA', 'a2_pe%d' % sl_], w=[pk(4 + hb)])
                                for hb in range(2):
                                    head = qc + 4 * hb
                                    ob = bank(4 + hb)
                                    (lambda ob: P.dve(lambda e: e.reciprocal(out=rec[64:128, :], in_=ob[64:128, :]),
                                                      r=[pk(4 + hb)], w=['a2_rec']))(ob)
                                    tt('dve', aT[:, head, :], ob[0:64, :], rec[64:128, :], ALU.mult,
                                       r=[pk(4 + hb), 'a2_rec'], w=['a2_aT'])
                            for g_ in range(4):
                                mm(bank(0), w_gr[:, g_, :], pld4[g_][:], True, True, r=['w_gr', 'a2_pld%d' % g_], w=[pk(0)])
                                ts('dve', plT[:, g_, :], bank(0), pscale[:, g_:g_ + 1], None, ALU.mult, None,
                                   r=[pk(0), 'pscale'], w=['a2_plT'])
                            def y_emit_a2(t_):
                                def f(yb):
                                    for n in range(2):
                                        for hh in range(8):
                                            mm(bank(yb + n), aT[:, hh, t_ * 128:(t_ + 1) * 128], w_oa[:, hh, n * 512:(n + 1) * 512],
                                               hh == 0, False, r=['a2_aT', 'w_oa'], w=[pk(yb + n)])
                                        for g_ in range(4):
                                            mm(bank(yb + n), plT[:, g_, t_ * 128:(t_ + 1) * 128], w_op[:, g_, n * 512:(n + 1) * 512],
                                               False, g_ == 3, r=['a2_plT', 'w_op'], w=[pk(yb + n)])
                                return f
                            for tp in range(2):
                                gens = []
                                for L in range(2):
                                    t_ = tp * 2 + L
                                    ti = cj * 4 + t_
                                    gens.append(mixer_post_g(ti, x, y_emit_a2(t_), mod0, 'mod0', ln0, wk[L], wr0,
                                                             6 if L == 0 else 2, 4 if L == 0 else 0))
                                lockstep(gens)
            if stop not in ('A1', 'A2'):
                with scope() as st:
                    phase_route(0, st)
            if stop not in ('A1', 'A2', 'C0'):
                with scope() as st:
                    phase_experts(0, st)
                with scope() as st:
                    ln0f = load_ln_rows(0, st, '0', 'f')
                    phase_ffn_post(0, H, mod0, 'mod0', ln0f, H2, 'H2_', st)
        if debug:
            final_keys += ['d_y', 'd_afft', 'd_idx', 'd_gate', 'F'] + ['H%d' % t for t in range(NT)] + ['U2_%d' % t for t in range(NT)]

        if stop is None or stop in ('F', 'G', 'C1'):
            with scope() as L1:
                mod1 = sb("mod1", [128, 6, D], F32, L1)
                with scope() as st:
                    phase_mod(1, c_col, mod1, 12, 'mod1', st)
                wr1 = sb("wr1", [128, 8, NE], F32, L1)
                dma_in(wr1[:], w_router[1].rearrange("(k p) e -> p k e", p=128), ['wr'])
                with scope() as Fs:
                    uTa = sb("uTa", [128, 8, S], BF16, Fs)
                    with scope() as st:
                        ht = [sb("f0_h%d" % i, [128, D], F32, st) for i in range(2)]
                        ub = [sb("f0_ub%d" % i, [128, D], BF16, st) for i in range(2)]
                        for ti in range(NT):
                            s_ = ti % 2
                            dma_in(ht[s_][:], H2[ti * 128:(ti + 1) * 128, :], ['f0_h%d' % s_], ['H2_%d' % ti])
                            tt('pool', ht[s_][:], ht[s_][:], mod1[:, 1, :], ALU.mult, r=['f0_h%d' % s_, 'mod1'], w=['f0_h%d' % s_])
                            tt('dve', ub[s_][:], ht[s_][:], mod1[:, 0, :], ALU.add, r=['f0_h%d' % s_, 'mod1'], w=['f0_ub%d' % s_])
                            pT = bank(s_).bitcast(BF16)
                            for k in range(8):
                                tr(pT[:, k * 128:(k + 1) * 128], ub[s_][:, k * 128:(k + 1) * 128], ident_bf[:],
                                   r=['f0_ub%d' % s_, 'ident_bf'], w=[pk(s_)])
                            cp('act', uTa[:, :, ti * 128:(ti + 1) * 128], pT.rearrange("p (k t) -> p k t", k=8), r=[pk(s_)], w=['uTa'])
                    with scope() as st:
                        wci = [sb("f_wci%d" % i, [128, 8, 3, 128], BF16, st) for i in range(2)]
                        cx = sb("f_cx", [128, S + 2], F32, st)
                        z = sb("f_z", [128, S], F32, st)
                        csb = [sb("f_c%d" % i, [128, 512], F32, st) for i in range(2)]
                        bzc = [sb("f_bz%d" % i, [128, 512], BF16, st) for i in range(2)]
                        P.dve(lambda e: e.memset(cx[:, 0:1], 0.0), w=['f_cx'])
                        P.dve(lambda e: e.memset(cx[:, S + 1:S + 2], 0.0), w=['f_cx'])
                        wv = w_conv_in.rearrange("(k p) (j n) -> p k j n", p=128, j=3)
                        for i in range(8):
                            s_ = i % 2
                            for j3 in range(3):
                                dma_in(wci[s_][:, :, j3, :], wv[:, :, j3, i * 128:(i + 1) * 128], ['f_wci%d' % s_], q='pool')
                            for tc in range(8):
                                bc, bx = 2 + (tc % 2) * 2, 3 + (tc % 2) * 2
                                for k in range(8):
                                    mm(bank(bc), wci[s_][:, k, 1, :], uTa[:, k, tc * 512:(tc + 1) * 512], k == 0, k == 7,
                                       r=['f_wci%d' % s_, 'uTa'], w=[pk(bc)])
                                for k in range(8):
                                    mm(bank(bx), wci[s_][:, k, 2, :], uTa[:, k, tc * 512:(tc + 1) * 512], k == 0, k == 7,
                                       r=['f_wci%d' % s_, 'uTa'], w=[pk(bx)])
                                cp('act', csb[tc % 2][:], bank(bc), r=[pk(bc)], w=['f_c%d' % (tc % 2)])
                                tt('dve', cx[:, 1 + tc * 512: 1 + (tc + 1) * 512], csb[tc % 2][:], bank(bx), ALU.mult,
                                   r=['f_c%d' % (tc % 2), pk(bx)], w=['f_cx'])
                            ts('dve', z[:], cx[:, 0:S], convw[:, i, 0:1], None, ALU.mult, None, r=['f_cx', 'convw'], w=['f_z'])
                            stt(z[:], cx[:, 1:S + 1], convw[:, i, 1:2], z[:], ALU.mult, ALU.add, r=['f_cx', 'convw', 'f_z'], w=['f_z'])
                            stt(z[:], cx[:, 2:S + 2], convw[:, i, 2:3], z[:], ALU.mult, ALU.add, r=['f_cx', 'convw', 'f_z'], w=['f_z'])
                            for tc in range(8):
                                bb = 6 + (tc % 2)
                                for k in range(8):
                                    mm(bank(bb), wci[s_][:, k, 0, :], uTa[:, k, tc * 512:(tc + 1) * 512], k == 0, k == 7,
                                       r=['f_wci%d' % s_, 'uTa'], w=[pk(bb)])
                                tt('dve', bzc[tc % 2][:], bank(bb), z[:, tc * 512:(tc + 1) * 512], ALU.mult,
                                   r=[pk(bb), 'f_z'], w=['f_bz%d' % (tc % 2)])
                                dma_in(BZ[:, i, tc * 512:(tc + 1) * 512], bzc[tc % 2][:], ['BZ'], ['f_bz%d' % (tc % 2)], q='pool')
                with scope() as st:
                    w_co = sb("w_co", [128, 8, D], BF16, st)
                    dma_in(w_co[:], w_conv_out.rearrange("(k p) n -> p k n", p=128), ['w_co'], q='pool')
                    for k in range(8):
                        tt('pool', w_co[:, k, :], w_co[:, k, :], mod1[:, 2, :], ALU.mult, r=['w_co', 'mod1'], w=['w_co'])
                    bzt = [sb("g_bz%d" % i, [128, 8, 128], BF16, st) for i in range(4)]
                    wk = alloc_post_work(st, 4)
                    ln1 = load_ln_rows(1, st, '1', 'm')
                    def y_emit_g(s_):
                        def f(yb):
                            for n in range(2):
                                for i in range(8):
                                    mm(bank(yb + n), bzt[s_][:, i, :], w_co[:, i, n * 512:(n + 1) * 512],
                                       i == 0, i == 7, r=['g_bz%d' % s_, 'w_co'], w=[pk(yb + n)])
                        return f
                    for W in wk:
                        W['res_keys'] = lambda t: ['H2_%d' % t]
                    for tp in range(NT // 4):
                        gens = []
                        for L in range(4):
                            ti = tp * 4 + L
                            dma_in(bzt[L][:], BZ[:, :, ti * 128:(ti + 1) * 128], ['g_bz%d' % L], ['BZ'])
                            gens.append(mixer_post_g(ti, H2, y_emit_g(L), mod1, 'mod1', ln1, wk[L], wr1, 2 * L, 2 * L))
                        lockstep(gens)
                if stop not in ('F', 'G'):
                    with scope() as st:
                        phase_route(1, st)
                if stop not in ('F', 'G', 'C1'):
                    with scope() as st:
                        phase_experts(1, st)
                    with scope() as st:
                        ln1f = load_ln_rows(1, st, '1', 'f')
                        phase_ffn_post(1, H, mod1, 'mod1', ln1f, out, 'out', st)
        P.emit(es, final_keys=final_keys)
    return nc, P


_CACHE = {}
CORE_SAMPLE = {0: 0, 1: 1, 2: 2, 3: 3}


def _consts():
    ident = np.eye(128, dtype=np.float32)
    bones = np.zeros((128, 128), np.float32)
    bones[0:64, 0:64] = 1.0
    bones[64:128, 64:128] = 1.0
    perm = np.zeros((128, 128), np.float32)
    for i in range(64):
        perm[2 * i + 1, 2 * i] = -1.0
        perm[2 * i, 2 * i + 1] = 1.0
    t = np.arange(S)
    row = (t // 64).astype(np.float32)
    col = (t % 64).astype(np.float32)
    inv_freq = (np.float32(10000.0) ** (-np.arange(0, 32, 2, dtype=np.float32) / np.float32(32))).astype(np.float32)
    ang = np.concatenate([row[:, None] * inv_freq[None, :], col[:, None] * inv_freq[None, :]], axis=-1).astype(np.float32)
    pair = (np.arange(128) % 64) // 2
    cosT = np.cos(ang).astype(np.float32).T[pair]
    sinT = np.sin(ang).astype(np.float32).T[pair]
    invcnt = np.zeros((4, S), np.float32)
    for g, w in enumerate(WINS):
        lo = np.maximum(t - w // 2, 0)
        hi = np.minimum(t + w // 2, S)
        invcnt[g] = (1.0 / (hi - lo).astype(np.float32)).astype(np.float32)
    bf = ml_dtypes.bfloat16
    return dict(ident_bf=ident.astype(bf), ident_f=ident, bones=bones.astype(bf), perm=perm.astype(bf),
                cosT=np.ascontiguousarray(cosT), sinT=np.ascontiguousarray(sinT), invcnt=invcnt)


def make_in_maps(inputs, ncores=8):
    f = lambda a: np.ascontiguousarray(np.asarray(a, dtype=np.float32))
    w_mix_in = f(inputs["w_mix_in"])[0]
    qcols = []
    for qc in range(4):
        qcols += list(range(qc * 64, qc * 64 + 64)) + list(range((4 + qc) * 64, (4 + qc) * 64 + 64))
    cols = qcols + list(range(512, 1280))
    w_mix_in_p = np.ascontiguousarray(w_mix_in[:, cols])
    shared = dict(
        w_mod=f(inputs["w_mod"]), b_mod=f(inputs["b_mod"]),
        ln_mix_g=f(inputs["ln_mix_g"]), ln_mix_b=f(inputs["ln_mix_b"]), ln_ffn_g=f(inputs["ln_ffn_g"]), ln_ffn_b=f(inputs["ln_ffn_b"]),
        w_mix_in=w_mix_in_p,
        gq_col=np.ascontiguousarray(np.tile(f(inputs["q_norm_g"])[0], 2).reshape(128, 1)),
        gk_col=np.ascontiguousarray(np.tile(f(inputs["k_norm_g"])[0], 2).reshape(128, 1)),
        w_pool_grp=f(inputs["w_pool_grp"])[0],
        pscale_col=np.ascontiguousarray(f(inputs["pool_scale"])[0].reshape(4, 128).T),
        w_mix_out=f(inputs["w_mix_out"])[0],
        w_conv_in=f(inputs["w_conv_in"])[0],
        convw_col=np.ascontiguousarray(f(inputs["conv_w"])[0].reshape(3, 8, 128).transpose(2, 1, 0)),
        w_conv_out=f(inputs["w_conv_out"])[0],
        w_router=f(inputs["w_router"]),
        w_exp_gate=f(inputs["w_exp_gate"]), w_exp_up=f(inputs["w_exp_up"]), w_exp_down=f(inputs["w_exp_down"]),
        cc_col=np.ascontiguousarray(f(inputs["c_ctx"]).reshape(8, 128).T),
    )
    shared.update(_consts())
    xs = f(inputs["x"]); cs = f(inputs["c"]); ctxs = f(inputs["ctx"])
    maps = []
    for i in range(ncores):
        b = i % 4
        m = dict(shared)
        m["x"] = xs[b]
        m["ctx"] = ctxs[b]
        m["c_col"] = np.ascontiguousarray(cs[b].reshape(8, 128).T)
        maps.append(m)
    return maps


def kernel(**inputs):
    if "nc" not in _CACHE:
        _CACHE["nc"] = build()[0]
    nc = _CACHE["nc"]
    maps = make_in_maps(inputs, 8)
    res = run_bass_kernel_spmd(nc, maps, core_ids=list(range(8)))
    core_of = {b: c for c, b in CORE_SAMPLE.items()}
    return np.stack([np.asarray(res.results[core_of[b]]["out"], dtype=np.float32) for b in range(4)], axis=0)
```

```python
import numpy as np
import ml_dtypes
from contextlib import ExitStack, contextmanager
import concourse.bass as bass
import concourse.mybir as mybir
from concourse.bass_utils import run_bass_kernel_spmd

F32 = mybir.dt.float32
BF16 = mybir.dt.bfloat16
I32 = mybir.dt.int32
U32 = mybir.dt.uint32
ALU = mybir.AluOpType
AF = mybir.ActivationFunctionType
AX = mybir.AxisListType

NDMA_SEM = 16
D = 1024
S = 4096
LC = 256
NT = S // 128
NKT = (S + LC) // 128
NE = 16
CAP = 512
ALPHA = float(4.0 ** 0.25)
EPS = 1e-6
WINS = (2, 4, 8, 16)


class Prog:
    def __init__(self, nc):
        self.nc = nc
        self.ops = []
        self.last_w = {}
        self.readers = {}

    def add(self, eng, fn, r=(), w=(), dma=False):
        i = len(self.ops)
        deps = set()
        for k in r:
            if k in self.last_w:
                deps.add(self.last_w[k])
        for k in w:
            if k in self.last_w:
                deps.add(self.last_w[k])
            deps.update(self.readers.get(k, ()))
        for k in r:
            self.readers.setdefault(k, []).append(i)
        for k in w:
            self.last_w[k] = i
            self.readers[k] = []
        deps.discard(i)
        self.ops.append(dict(eng=eng, fn=fn, deps=deps, dma=dma))
        return i

    def barrier(self):
        start = getattr(self, '_bar_start', 0)
        deps = set()
        last = {}
        for i in range(start, len(self.ops)):
            o = self.ops[i]
            if o['fn'] is None:
                continue
            if o['dma']:
                deps.add(i)
            else:
                last[o['eng']] = i
        deps.update(last.values())
        for e in ('pe', 'act', 'dve', 'pool', 'sp'):
            self.ops.append(dict(eng=e, fn=None, deps=set(deps), dma=False))
        self._bar_start = len(self.ops)

    def pe(self, fn, r=(), w=()):
        return self.add('pe', fn, r, w)

    def act(self, fn, r=(), w=()):
        return self.add('act', fn, r, w)

    def dve(self, fn, r=(), w=()):
        return self.add('dve', fn, r, w)

    def pool(self, fn, r=(), w=()):
        return self.add('pool', fn, r, w)

    def dma(self, fn, r=(), w=(), q='sp'):
        return self.add(q, fn, r, w, dma=True)

    def emit(self, es, final_keys=()):
        nc = self.nc
        ops = self.ops
        self.add('sp', None, r=list(final_keys), w=())
        n = len(ops)
        has_dep = [False] * n
        for i, o in enumerate(ops):
            for d in o['deps']:
                if ops[d]['eng'] == 'pe' and o['eng'] == 'pe' and not ops[d]['dma'] and not o['dma']:
                    continue
                has_dep[d] = True
        engs = ['pe', 'act', 'dve', 'pool', 'sp']
        esem = {e: es.enter_context(nc.semaphore("s_" + e)) for e in engs}
        dsem = {q: [es.enter_context(nc.semaphore("d%s%d" % (q, j))) for j in range(NDMA_SEM)] for q in ('sp', 'pool')}
        ecount = {e: 0 for e in engs}
        dcount = {q: [0] * NDMA_SEM for q in ('sp', 'pool')}
        dnext = {'sp': 0, 'pool': 0}
        sig = [None] * n
        prevdma = [None] * n
        for i, o in enumerate(ops):
            if o['fn'] is None:
                continue
            if o['dma']:
                q = o['eng']
                j = dnext[q] % NDMA_SEM
                dnext[q] += 1
                if dcount[q][j] > 0:
                    prevdma[i] = (dsem[q][j], dcount[q][j] * 16, ('d', q, j))
                dcount[q][j] += 1
                sig[i] = (dsem[q][j], dcount[q][j] * 16, ('d', q, j))
            elif has_dep[i]:
                e = o['eng']
                ecount[e] += 1
                sig[i] = (esem[e], ecount[e], ('e', e))
        per_eng = {e: [i for i, o in enumerate(ops) if o['eng'] == e] for e in engs}
        self.stats = {e: len(per_eng[e]) for e in engs}

        def run_engine(ename, eobj):
            waited = {}
            for i in per_eng[ename]:
                o = ops[i]
                need = []
                for d in sorted(o['deps']):
                    if ops[d]['eng'] == 'pe' and ename == 'pe' and not ops[d]['dma'] and not o['dma']:
                        continue
                    need.append(sig[d])
                if prevdma[i] is not None:
                    need.append(prevdma[i])
                for s in need:
                    if s is None:
                        continue
                    sem, val, key = s
                    if waited.get(key, 0) < val:
                        eobj.wait_ge(sem, val)
                        waited[key] = val
                if o['fn'] is None:
                    continue
                ins = o['fn'](eobj)
                if sig[i] is not None:
                    sem, val, key = sig[i]
                    ins.then_inc(sem, 16 if o['dma'] else 1)

        with nc.Block() as block:
            @block.tensor
            def _(e):
                run_engine('pe', e)

            @block.scalar
            def _(e):
                run_engine('act', e)

            @block.vector
            def _(e):
                run_engine('dve', e)

            @block.gpsimd
            def _(e):
                run_engine('pool', e)

            @block.sync
            def _(e):
                run_engine('sp', e)


def build(debug=False, stop=None):
    nc = bass.Bass("TRN2", target_bir_lowering=False)
    P = Prog(nc)

    @contextmanager
    def scope():
        with ExitStack() as st_:
            yield st_
        P.barrier()

    def din(name, shape, dt=F32):
        return nc.dram_tensor(name, list(shape), dt, kind="ExternalInput").ap()

    x = din("x", [S, D]); ctx = din("ctx", [LC, D])
    c_col = din("c_col", [128, 8]); cc_col = din("cc_col", [128, 8])
    w_mod = din("w_mod", [2, D, 6 * D]); b_mod = din("b_mod", [2, 6 * D])
    ln_mix_g = din("ln_mix_g", [2, D]); ln_mix_b = din("ln_mix_b", [2, D])
    ln_ffn_g = din("ln_ffn_g", [2, D]); ln_ffn_b = din("ln_ffn_b", [2, D])
    w_mix_in = din("w_mix_in", [D, 1280])
    gq_col = din("gq_col", [128, 1]); gk_col = din("gk_col", [128, 1])
    w_pool_grp = din("w_pool_grp", [4, 128, 128]); pscale_col = din("pscale_col", [128, 4])
    w_mix_out = din("w_mix_out", [D, D])
    w_conv_in = din("w_conv_in", [D, 3 * D]); convw_col = din("convw_col", [128, 8, 3])
    w_conv_out = din("w_conv_out", [D, D])
    w_router = din("w_router", [2, D, NE])
    w_eg = din("w_exp_gate", [2, NE, D, D]); w_eu = din("w_exp_up", [2, NE, D, D]); w_ed = din("w_exp_down", [2, NE, D, D])
    ident_bf_d = din("ident_bf", [128, 128], BF16); ident_f_d = din("ident_f", [128, 128])
    bones_d = din("bones", [128, 128], BF16); perm_d = din("perm", [128, 128], BF16)
    cosT = din("cosT", [128, S]); sinT = din("sinT", [128, S]); invcnt = din("invcnt", [4, S])
    out = nc.dram_tensor("out", [S, D], F32, kind="ExternalOutput").ap()
    kind_s = "ExternalOutput" if debug else "Internal"

    def dscr(name, shape, dt=F32):
        return nc.dram_tensor(name, list(shape), dt, kind=kind_s).ap()

    H = dscr("H", [S, D]); H2 = dscr("H2", [S, D]); U2 = dscr("U2", [S, D], BF16); Fb = dscr("Fb", [S, D])
    PT = dscr("PT", [4, 128, S + 16])
    BZ = dscr("BZ", [128, 8, S], BF16)
    AFD = dscr("AFD", [16, S]); CD = dscr("CD", [128, 128])
    dbg = {}
    if debug:
        dbg["d_mod0"] = nc.dram_tensor("d_mod0", [128, 6 * D], F32, kind="ExternalOutput").ap()
        dbg["d_modc"] = nc.dram_tensor("d_modc", [128, 2 * D], F32, kind="ExternalOutput").ap()
        dbg["d_qt"] = nc.dram_tensor("d_qt", [128, 4 * S], BF16, kind="ExternalOutput").ap()
        dbg["d_kt"] = nc.dram_tensor("d_kt", [128, S + LC], BF16, kind="ExternalOutput").ap()
        dbg["d_va"] = nc.dram_tensor("d_va", [128, NKT * 256], BF16, kind="ExternalOutput").ap()
        dbg["d_afft"] = nc.dram_tensor("d_afft", [2, 16, S], F32, kind="ExternalOutput").ap()
        dbg["d_idx"] = nc.dram_tensor("d_idx", [2, 128, 64], I32, kind="ExternalOutput").ap()
        dbg["d_gate"] = nc.dram_tensor("d_gate", [2, 128, 64], F32, kind="ExternalOutput").ap()
        dbg["d_y"] = nc.dram_tensor("d_y", [2, S, D], F32, kind="ExternalOutput").ap()

    final_keys = ['out%d' % t for t in range(NT)]
    _uid = [0]
    with ExitStack() as es:
        def sb(name, shape, dt, st=es):
            _uid[0] += 1
            t_ = st.enter_context(nc.sbuf_tensor("sb%d_%s" % (_uid[0], name), list(shape), dt))
            P.min_free = min(getattr(P, 'min_free', 1 << 30), nc.sbuf_bytes_remaining)
            return t_

        PS = es.enter_context(nc.psum_tensor("PS", [128, 4096], F32))

        def bank(i, n=1):
            return PS[:, i * 512:(i + n) * 512]

        def pk(i):
            return 'ps%d' % i

        ident_bf = sb("ident_bf", [128, 128], BF16); ident_f = sb("ident_f", [128, 128], F32)
        bones = sb("bones", [128, 128], BF16); perm = sb("perm", [128, 128], BF16)
        gq = sb("gq", [128, 1], F32); gk = sb("gk", [128, 1], F32)
        pscale = sb("pscale", [128, 4], F32); convw = sb("convw", [128, 8, 3], F32)
        ones_f = sb("ones_f", [128, 128], F32)
        affT = sb("affT", [16, S], F32)
        idxT = sb("idxT", [128, 4, NE], I32); gateT = sb("gateT", [128, 4, NE], F32)
        for (t, src, key) in [(ident_bf, ident_bf_d, 'ident_bf'), (ident_f, ident_f_d, 'ident_f'), (bones, bones_d, 'bones'),
                              (perm, perm_d, 'perm'), (gq, gq_col, 'gq'), (gk, gk_col, 'gk'), (pscale, pscale_col, 'pscale'),
                              (convw, convw_col, 'convw')]:
            (lambda t, src, key: P.dma(lambda e: e.dma_start(out=t[:], in_=src), w=[key]))(t, src, key)
        P.dve(lambda e: e.memset(ones_f[:], 1.0), w=['ones_f'])
        epsc = sb("epsc", [128, 1], F32)
        P.dve(lambda e: e.memset(epsc[:], EPS), w=['epsc'])

        def dma_in(dst, src, wkeys, rkeys=(), q='sp'):
            P.dma(lambda e: e.dma_start(out=dst, in_=src), r=list(rkeys), w=list(wkeys), q=q)

        def mm(o, lhsT, rhs, start, stop, r, w):
            P.pe(lambda e: e.matmul(o, lhsT=lhsT, rhs=rhs, start=start, stop=stop), r=r, w=w)

        def tr(o, in_, ident, r, w):
            P.pe(lambda e: e.transpose(o, in_, ident), r=r, w=w)

        def tt(eng, o, a, b, op, r, w):
            P.add(eng, lambda e: e.tensor_tensor(out=o, in0=a, in1=b, op=op), r=r, w=w)

        def ts(eng, o, a, s1, s2, op0, op1, r, w):
            if op1 is None:
                P.add(eng, lambda e: e.tensor_scalar(out=o, in0=a, scalar1=s1, scalar2=None, op0=op0), r=r, w=w)
            else:
                P.add(eng, lambda e: e.tensor_scalar(out=o, in0=a, scalar1=s1, scalar2=s2, op0=op0, op1=op1), r=r, w=w)

        def stt(o, a, sc, b, op0, op1, r, w):
            P.dve(lambda e: e.scalar_tensor_tensor(out=o, in0=a, scalar=sc, in1=b, op0=op0, op1=op1), r=r, w=w)

        def actf(o, a, func, r, w, scale=None, bias=None, accum=None):
            kw = {}
            if scale is not None:
                kw['scale'] = scale
            if bias is not None:
                kw['bias'] = bias
            if accum is not None:
                kw['accum_out'] = accum
            P.act(lambda e: e.activation(out=o, in_=a, func=func, **kw), r=r, w=w)

        def cp(eng, o, a, r, w):
            if eng == 'act':
                P.act(lambda e: e.copy(out=o, in_=a), r=r, w=w)
            else:
                P.add(eng, lambda e: e.tensor_copy(out=o, in_=a), r=r, w=w)

        def phase_mod(layer, cond_col, modbuf, nchunks, tag, st):
            cond = sb("cond" + tag, [128, 8], F32, st)
            condB = sb("condB" + tag, [128, 8, 128], BF16, st)
            dma_in(cond[:], cond_col, ['cond' + tag])
            actf(cond[:], cond[:], AF.Silu, r=['cond' + tag], w=['cond' + tag])
            for k in range(8):
                ts('dve', condB[:, k, :], ones_f[:], cond[:, k:k + 1], None, ALU.mult, None,
                   r=['cond' + tag, 'ones_f'], w=['condB' + tag])
            wm = [sb("wm%d" % s_ + tag, [128, 8, 512], BF16, st) for s_ in range(4)]
            bm = [sb("bm%d" % s_ + tag, [128, 512], F32, st) for s_ in range(4)]
            wv = w_mod[layer].rearrange("(k p) n -> p k n", p=128)
            for n in range(nchunks):
                s_ = n % 4
                j, half = n // 2, n % 2
                dma_in(wm[s_][:], wv[:, :, n * 512:(n + 1) * 512], ['wm%d' % s_ + tag], q='pool')
                dma_in(bm[s_][:], b_mod[layer:layer + 1, n * 512:(n + 1) * 512].broadcast_to([128, 512]), ['bm%d' % s_ + tag])
                b_ = 6 + (s_ % 2)
                for k in range(8):
                    mm(bank(b_), condB[:, k, :], wm[s_][:, k, :], k == 0, k == 7,
                       r=['condB' + tag, 'wm%d' % s_ + tag], w=[pk(b_)])
                o = modbuf[:, j, half * 512:(half + 1) * 512]
                if j in (1, 4):
                    stt(o, bank(b_), 1.0, bm[s_][:], ALU.add, ALU.add, r=[pk(b_), 'bm%d' % s_ + tag], w=[tag])
                else:
                    tt('dve', o, bank(b_), bm[s_][:], ALU.add, r=[pk(b_), 'bm%d' % s_ + tag], w=[tag])

        def load_ln_rows(layer, st, tag, which):
            rows = {}
            for nm, src in ((('mg', ln_mix_g), ('mb', ln_mix_b)) if which == 'm' else (('fg', ln_ffn_g), ('fb', ln_ffn_b))):
                t = sb("ln_" + nm + tag, [128, D], F32, st)
                dma_in(t[:], src[layer:layer + 1, :].broadcast_to([128, D]), ['ln_' + nm + tag])
                rows[nm] = (t, 'ln_' + nm + tag)
            return rows

        def lockstep(gens):
            gens = list(gens)
            while gens:
                for g_ in list(gens):
                    try:
                        next(g_)
                    except StopIteration:
                        gens.remove(g_)

        def layer_norm_g(t2, k_t2, grow, brow, stats, k_st, o, k_o, aff_eng='dve'):
            for hh in range(2):
                (lambda hh: P.dve(lambda e: e.bn_stats(out=stats[:, hh * 6:(hh + 1) * 6], in_=t2[:, hh * 512:(hh + 1) * 512]),
                                  r=[k_t2], w=[k_st]))(hh)
            P.dve(lambda e: e.bn_aggr(out=stats[:, 12:14], in_=stats[:, 0:12]), r=[k_st], w=[k_st])
            yield
            actf(stats[:, 14:15], stats[:, 13:14], AF.Sqrt, r=[k_st, 'epsc'], w=[k_st], scale=1.0, bias=epsc[:, 0:1])
            yield
            P.dve(lambda e: e.reciprocal(out=stats[:, 14:15], in_=stats[:, 14:15]), r=[k_st], w=[k_st])
            ts('dve', stats[:, 15:16], stats[:, 12:13], stats[:, 14:15], -1.0, ALU.mult, ALU.mult, r=[k_st], w=[k_st])
            yield
            actf(t2, t2, AF.Identity, r=[k_t2, k_st], w=[k_t2], scale=stats[:, 14:15], bias=stats[:, 15:16])
            yield
            tt(aff_eng, t2, t2, grow[0][:], ALU.mult, r=[k_t2, grow[1]], w=[k_t2])
            tt(aff_eng, o, t2, brow[0][:], ALU.add, r=[k_t2, brow[1]], w=[k_o])
            yield

        def mixer_post_g(ti, res_src, y_emit, modbuf, k_mod, ln, W, wr_sb, yb, rb):
            L = W['lane']
            t1, stats, h1, u2f, u2b, u2T, sm = W['t1'], W['stats'], W['h1'], W['u2f'], W['u2b'], W['u2T'], W['sm']
            k_t1, k_st, k_h1, k_u2f, k_u2b, k_u2T, k_sm = ['pw_%s%d' % (n_, L) for n_ in ('t1', 'st', 'h1', 'u2f', 'u2b', 'u2T', 'sm')]
            dma_in(t1[:], res_src[ti * 128:(ti + 1) * 128, :], [k_t1], W.get('res_keys', lambda t: [])(ti))
            y_emit(yb)
            yield
            stt(t1[:], t1[:], ALPHA, bank(yb, 2), ALU.mult, ALU.add, r=[k_t1, pk(yb), pk(yb + 1)], w=[k_t1])
            yield
            for _ in layer_norm_g(t1[:], k_t1, ln['mg'], ln['mb'], stats, k_st, h1[:], k_h1):
                yield
            dma_in(H[ti * 128:(ti + 1) * 128, :], h1[:], ['H%d' % ti], [k_h1], q='pool')
            tt('dve', u2f[:], h1[:], modbuf[:, 4, :], ALU.mult, r=[k_h1, k_mod], w=[k_u2f])
            tt('dve', u2f[:], u2f[:], modbuf[:, 3, :], ALU.add, r=[k_u2f, k_mod], w=[k_u2f])
            yield
            cp('act', u2b[:], u2f[:], r=[k_u2f], w=[k_u2b])
            dma_in(U2[ti * 128:(ti + 1) * 128, :], u2b[:], ['U2_%d' % ti], [k_u2b], q='pool')
            for half in range(2):
                b_ = rb + half
                for kk in range(4):
                    k = half * 4 + kk
                    tr(bank(b_)[:, kk * 128:(kk + 1) * 128], u2f[:, k * 128:(k + 1) * 128], ident_f[:],
                       r=[k_u2f, 'ident_f'], w=[pk(b_)])
            yield
            for half in range(2):
                b_ = rb + half
                cp('act' if half == 0 else 'dve', u2T[:, half * 4:(half + 1) * 4, :],
                   bank(b_).rearrange("p (k t) -> p k t", k=4), r=[pk(b_)], w=[k_u2T])
            yield
            lg = bank(rb)[:, 0:NE]
            for k in range(8):
                mm(lg, u2T[:, k, :], wr_sb[:, k, :], k == 0, k == 7, r=[k_u2T, 'wr'], w=[pk(rb)])
            yield
            P.dve(lambda e: e.tensor_reduce(out=sm[:, 16:17], in_=lg, axis=AX.X, op=ALU.max), r=[pk(rb)], w=[k_sm])
            ts('dve', sm[:, 17:18], sm[:, 16:17], -1.0, None, ALU.mult, None, r=[k_sm], w=[k_sm])
            yield
            actf(sm[:, 0:16], lg, AF.Exp, r=[pk(rb), k_sm], w=[k_sm], bias=sm[:, 17:18], scale=1.0)
            yield
            P.dve(lambda e: e.tensor_reduce(out=sm[:, 18:19], in_=sm[:, 0:16], axis=AX.X, op=ALU.add), r=[k_sm], w=[k_sm])
            P.dve(lambda e: e.reciprocal(out=sm[:, 19:20], in_=sm[:, 18:19]), r=[k_sm], w=[k_sm])
            ts('dve', sm[:, 0:16], sm[:, 0:16], sm[:, 19:20], None, ALU.mult, None, r=[k_sm], w=[k_sm])
            yield
            tr(bank(rb + 1)[0:16, 0:128], sm[:, 0:16], ident_f[:], r=[k_sm, 'ident_f'], w=[pk(rb + 1)])
            yield
            cp('dve', affT[:, ti * 128:(ti + 1) * 128], bank(rb + 1)[0:16, 0:128], r=[pk(rb + 1)], w=['affT'])
            yield

        def alloc_post_work(st, nl=2):
            lanes = []
            for L in range(nl):
                W = dict(lane=L)
                W['t1'] = sb("pw_t1_%d" % L, [128, D], F32, st)
                W['stats'] = sb("pw_st_%d" % L, [128, 16], F32, st)
                W['h1'] = sb("pw_h1_%d" % L, [128, D], F32, st)
                W['u2f'] = sb("pw_u2f_%d" % L, [128, D], F32, st)
                W['u2b'] = sb("pw_u2b_%d" % L, [128, D], BF16, st)
                W['u2T'] = sb("pw_u2T_%d" % L, [128, 8, 128], F32, st)
                W['sm'] = sb("pw_sm_%d" % L, [128, 20], F32, st)
                lanes.append(W)
            return lanes

        def phase_route(layer, st):
            vals = sb("rt_vals", [16, CAP], F32, st)
            idxu = sb("rt_idxu", [16, CAP], U32, st)
            idxf = sb("rt_idxf", [16, CAP], F32, st)
            cand = sb("cand", [16, 1024], F32, st)
            a128 = sb("rt_a128", [128, 512], F32, st)
            c128 = sb("rt_c128", [128, 128], F32, st)
            if debug:
                dma_in(dbg["d_afft"][layer], affT[:], ['d_afft'], ['affT'], q='pool')
            dma_in(AFD, affT[:], ['AFD'], ['affT'])
            dma_in(a128[:], AFD.rearrange("e (c j) -> (e c) j", j=512), ['rt_a128'], ['AFD'])
            for r_ in range(16):
                sl = slice(r_ * 8, (r_ + 1) * 8)
                (lambda sl: P.dve(lambda e: e.max(out=c128[:, sl], in_=a128[:]), r=['rt_a128'], w=['rt_c128']))(sl)
                (lambda sl: P.dve(lambda e: e.match_replace(out=a128[:], in_to_replace=c128[:, sl], in_values=a128[:], imm_value=-1.0),
                                  r=['rt_a128', 'rt_c128'], w=['rt_a128']))(sl)
            dma_in(CD, c128[:], ['CD'], ['rt_c128'])
            dma_in(cand[:], CD.rearrange("(e c) r -> e (c r)", c=8), ['cand'], ['CD'])
            for r_ in range(CAP // 8):
                sl = slice(r_ * 8, (r_ + 1) * 8)
                (lambda sl: P.dve(lambda e: e.max(out=vals[:, sl], in_=cand[:]), r=['cand'], w=['rt_vals']))(sl)
                (lambda sl: P.dve(lambda e: e.match_replace(out=cand[:], in_to_replace=vals[:, sl], in_values=cand[:], imm_value=-1.0),
                                  r=['cand', 'rt_vals'], w=['cand']))(sl)
                (lambda sl: P.dve(lambda e: e.max_index(out=idxu[:, sl], in_max=vals[:, sl], in_values=affT[:]),
                                  r=['affT', 'rt_vals'], w=['rt_idxu']))(sl)
            cp('dve', idxf[:], idxu[:], r=['rt_idxu'], w=['rt_idxf'])
            for stl in range(4):
                tr(bank(4)[:, stl * 16:(stl + 1) * 16], idxf[:, stl * 128:(stl + 1) * 128], ident_f[0:16, 0:16],
                   r=['rt_idxf', 'ident_f'], w=[pk(4)])
                tr(bank(5)[:, stl * 16:(stl + 1) * 16], vals[:, stl * 128:(stl + 1) * 128], ident_f[0:16, 0:16],
                   r=['rt_vals', 'ident_f'], w=[pk(5)])
            cp('dve', idxT[:].rearrange("p a b -> p (a b)"), bank(4)[:, 0:64], r=[pk(4)], w=['idxT'])
            cp('dve', gateT[:].rearrange("p a b -> p (a b)"), bank(5)[:, 0:64], r=[pk(5)], w=['gateT'])
            if debug:
                dma_in(dbg["d_idx"][layer], idxT[:].rearrange("p a b -> p (a b)"), ['d_idx'], ['idxT'], q='pool')
                dma_in(dbg["d_gate"][layer], gateT[:].rearrange("p a b -> p (a b)"), ['d_gate'], ['gateT'], q='pool')

        def phase_experts(layer, st):
            with scope() as stz:
                zt = sb("ex_zero", [128, D], F32, stz)
                P.pool(lambda e: e.memset(zt[:], 0.0), w=['ex_zero'])
                for ti in range(NT):
                    dma_in(Fb[ti * 128:(ti + 1) * 128, :], zt[:], ['Fz%d' % ti], ['ex_zero', 'Frd%d' % ti])
            wg = [sb("ex_wg%d" % i, [128, 8, D], BF16, st) for i in range(2)]
            wu = [sb("ex_wu%d" % i, [128, 8, D], BF16, st) for i in range(2)]
            wd = [sb("ex_wd%d" % i, [128, 8, D], BF16, st) for i in range(2)]
            xe = [sb("ex_xe%d" % i, [128, 4, D], BF16, st) for i in range(2)]
            xeT = [sb("ex_xeT%d" % i, [128, 8, CAP], BF16, st) for i in range(2)]
            actT = sb("ex_actT", [128, 8, CAP], BF16, st)
            sg = [sb("ex_sg%d" % i, [128, CAP], F32, st) for i in range(2)]
            yo = [sb("ex_yo%d" % i, [128, D], F32, st) for i in range(2)]
            u2_all_keys = ['U2_%d' % t for t in range(NT)]

            def load_w(e_, which):
                s_ = e_ % 2
                for (buf, src, nm) in ((wg, w_eg, 'wg'), (wu, w_eu, 'wu'), (wd, w_ed, 'wd')):
                    if nm in which:
                        dma_in(buf[s_][:], src[layer, e_].rearrange("(k p) f -> p k f", p=128), ['ex_%s%d' % (nm, s_)], q='pool')

            def gather(e_):
                s_ = e_ % 2
                for stl in range(4):
                    (lambda stl: P.dma(lambda e: e.indirect_dma_start(
                        out=xe[s_][:, stl, :], out_offset=None, in_=U2,
                        in_offset=bass.IndirectOffsetOnAxis(ap=idxT[:, stl, e_:e_ + 1], axis=0)),
                        r=['idxT'] + u2_all_keys, w=['ex_xe%d_%d' % (s_, stl)], q='pool'))(stl)

            def do_T(e_):
                s_ = e_ % 2
                for stl in range(4):
                    b_ = stl % 2
                    pT = bank(b_).bitcast(BF16)
                    for k in range(8):
                        tr(pT[:, k * 128:(k + 1) * 128], xe[s_][:, stl, k * 128:(k + 1) * 128], ident_bf[:],
                           r=['ex_xe%d_%d' % (s_, stl), 'ident_bf'], w=[pk(b_)])
                    cp('act' if stl % 2 == 0 else 'dve', xeT[s_][:, :, stl * 128:(stl + 1) * 128],
                       pT.rearrange("p (k t) -> p k t", k=8), r=[pk(b_)], w=['ex_xeT%d' % s_])

            def do_HGU(e_):
                s_ = e_ % 2
                for fc in range(8):
                    bg = 2 + (fc % 2) * 2
                    bu = bg + 1
                    for k in range(8):
                        mm(bank(bg), wg[s_][:, k, fc * 128:(fc + 1) * 128], xeT[s_][:, k, :], k == 0, k == 7,
                           r=['ex_wg%d' % s_, 'ex_xeT%d' % s_], w=[pk(bg)])
                    for k in range(8):
                        mm(bank(bu), wu[s_][:, k, fc * 128:(fc + 1) * 128], xeT[s_][:, k, :], k == 0, k == 7,
                           r=['ex_wu%d' % s_, 'ex_xeT%d' % s_], w=[pk(bu)])
                    actf(sg[fc % 2][:], bank(bg), AF.Silu, r=[pk(bg)], w=['ex_sg%d' % (fc % 2)])
                    tt('dve', actT[:, fc, :], sg[fc % 2][:], bank(bu), ALU.mult, r=['ex_sg%d' % (fc % 2), pk(bu)], w=['ex_actT'])

            def do_YE(e_):
                s_ = e_ % 2
                for stl in range(4):
                    yb = 6
                    for n in range(2):
                        for fc in range(8):
                            mm(bank(yb + n), actT[:, fc, stl * 128:(stl + 1) * 128], wd[s_][:, fc, n * 512:(n + 1) * 512],
                               fc == 0, fc == 7, r=['ex_actT', 'ex_wd%d' % s_], w=[pk(yb + n)])
                    y2 = bank(yb, 2)
                    if stl % 2 == 0:
                        actf(yo[stl % 2][:], y2, AF.Copy, r=[pk(yb), pk(yb + 1), 'gateT'], w=['ex_yo%d' % (stl % 2)],
                             scale=gateT[:, stl, e_:e_ + 1])
                    else:
                        ts('dve', yo[stl % 2][:], y2, gateT[:, stl, e_:e_ + 1], None, ALU.mult, None,
                           r=[pk(yb), pk(yb + 1), 'gateT'], w=['ex_yo%d' % (stl % 2)])
                    (lambda stl, e_: P.dma(lambda e: e.indirect_dma_start(
                        out=Fb, out_offset=bass.IndirectOffsetOnAxis(ap=idxT[:, stl, e_:e_ + 1], axis=0),
                        in_=yo[stl % 2][:, :], in_offset=None, compute_op=ALU.add),
                        r=['idxT', 'ex_yo%d' % (stl % 2)] + ['Fz%d' % t for t in range(NT)], w=['F'], q='pool'))(stl, e_)

            for e0 in range(2):
                load_w(e0, ('wg', 'wu', 'wd'))
                gather(e0)
            do_T(0)
            for e_ in range(NE):
                if e_ >= 1 and e_ + 1 < NE:
                    gather(e_ + 1)
                do_HGU(e_)
                if e_ + 2 < NE:
                    load_w(e_ + 2, ('wg', 'wu'))
                if e_ + 1 < NE:
                    do_T(e_ + 1)
                do_YE(e_)
                if e_ + 2 < NE:
                    load_w(e_ + 2, ('wd',))

        def phase_ffn_post(layer, src_h, modbuf, k_mod, ln, dst, dst_key, st):
            ht = [sb("fp_h%d" % i, [128, D], F32, st) for i in range(4)]
            ft = [sb("fp_f%d" % i, [128, D], F32, st) for i in range(4)]
            ot = [sb("fp_o%d" % i, [128, D], F32, st) for i in range(4)]
            stt_ = [sb("fp_s%d" % i, [128, 16], F32, st) for i in range(4)]

            def gen(ti, s_):
                dma_in(ht[s_][:], src_h[ti * 128:(ti + 1) * 128, :], ['fp_h%d' % s_], ['H%d' % ti])
                dma_in(ft[s_][:], Fb[ti * 128:(ti + 1) * 128, :], ['fp_f%d' % s_, 'Frd%d' % ti], ['F'])
                yield
                tt('dve', ft[s_][:], ft[s_][:], modbuf[:, 5, :], ALU.mult, r=['fp_f%d' % s_, k_mod], w=['fp_f%d' % s_])
                stt(ft[s_][:], ht[s_][:], ALPHA, ft[s_][:], ALU.mult, ALU.add, r=['fp_h%d' % s_, 'fp_f%d' % s_], w=['fp_f%d' % s_])
                yield
                for _ in layer_norm_g(ft[s_][:], 'fp_f%d' % s_, ln['fg'], ln['fb'], stt_[s_], 'fp_s%d' % s_, ot[s_][:], 'fp_o%d' % s_, 'pool'):
                    yield
                dma_in(dst[ti * 128:(ti + 1) * 128, :], ot[s_][:], [dst_key + '%d' % ti], ['fp_o%d' % s_], q='pool')
                yield

            for tp in range(NT // 4):
                lockstep([gen(tp * 4 + L, L) for L in range(4)])

        with scope() as L0:
            mod0 = sb("mod0", [128, 6, D], F32, L0)
            with scope() as st:
                phase_mod(0, c_col, mod0, 12, 'mod0', st)
            if debug:
                dma_in(dbg["d_mod0"], mod0[:].rearrange("p a b -> p (a b)"), ['d_mod0'], ['mod0'])
                final_keys += ['d_mod0', 'd_modc']
            wr0 = sb("wr0", [128, 8, NE], F32, L0)
            dma_in(wr0[:], w_router[0].rearrange("(k p) e -> p k e", p=128), ['wr'])

            with scope() as A:
                QT = sb("QT", [128, 4, S], BF16, A)
                KT = sb("KT", [128, S + LC], BF16, A)
                VA = sb("VA", [128, NKT, 2, 128], BF16, A)
                P.dve(lambda e: e.memset(VA[:].rearrange("p a b c -> p (a b c)"), 1.0), w=['VA'])
                with scope() as st:
                    zp = sb("zp", [128, 8], F32, st)
                    P.dve(lambda e: e.memset(zp[:], 0.0), w=['zp'])
                    for g_ in range(4):
                        dma_in(PT[g_, :, 0:8], zp[:], ['PT'], ['zp'])
                        dma_in(PT[g_, :, S + 8:S + 16], zp[:], ['PT'], ['zp'])

                with scope() as st:
                    modc = sb("modc", [128, 2, D], F32, st)
                    with scope() as st2:
                        phase_mod(0, cc_col, modc, 4, 'modc', st2)
                    if debug:
                        dma_in(dbg["d_modc"], modc[:].rearrange("p a b -> p (a b)"), ['d_modc'], ['modc'])
                    w_in = sb("w_in", [128, 8, 1280], BF16, st)
                    dma_in(w_in[:], w_mix_in.rearrange("(k p) n -> p k n", p=128), ['w_in'], q='pool')
                    xt = [sb("a1_x%d" % i, [128, D], F32, st) for i in range(2)]
                    ub = [sb("a1_ub%d" % i, [128, D], BF16, st) for i in range(2)]
                    uT = [sb("a1_uT%d" % i, [128, 8, 512], BF16, st) for i in range(2)]
                    cs = [sb("a1_cos%d" % i, [128, 512], F32, st) for i in range(2)]
                    sn = [sb("a1_sin%d" % i, [128, 512], F32, st) for i in range(2)]
                    lanes_t = [(sb("a1_sq%d" % L, [128, 512], BF16, st), sb("a1_rstd%d" % L, [128, 512], F32, st),
                                sb("a1_qn%d" % L, [128, 512], BF16, st), sb("a1_t1%d" % L, [128, 512], F32, st),
                                sb("a1_t2%d" % L, [128, 512], F32, st)) for L in range(2)]
                    pst = [sb("a1_pst%d" % i, [128, 512], F32, st) for i in range(4)]
                    tile_ctr = [0]

                    def make_uT(src, row0, ntiles, modb, k_modb, cj):
                        s2 = cj % 2
                        for t_ in range(ntiles):
                            s_ = tile_ctr[0] % 2
                            tile_ctr[0] += 1
                            dma_in(xt[s_][:], src[row0 + t_ * 128: row0 + (t_ + 1) * 128, :], ['a1_x%d' % s_])
                            tt('pool', xt[s_][:], xt[s_][:], modb[:, 1, :], ALU.mult, r=['a1_x%d' % s_, k_modb], w=['a1_x%d' % s_])
                            tt('pool', ub[s_][:], xt[s_][:], modb[:, 0, :], ALU.add, r=['a1_x%d' % s_, k_modb], w=['a1_ub%d' % s_])
                            b_ = s_
                            pT = bank(b_).bitcast(BF16)
                            for k in range(8):
                                tr(pT[:, k * 128:(k + 1) * 128], ub[s_][:, k * 128:(k + 1) * 128], ident_bf[:],
                                   r=['a1_ub%d' % s_, 'ident_bf'], w=[pk(b_)])
                            cp('act', uT[s2][:, :, t_ * 128:(t_ + 1) * 128], pT.rearrange("p (k t) -> p k t", k=8),
                               r=[pk(b_)], w=['a1_uT%d' % s2])

                    def proj_norm(cj, ncols, col0, gcol, k_g, rope, dst, k_dst, lane):
                        s2 = cj % 2
                        b_, ba = (2, 3) if lane == 0 else (6, 7)
                        sq_, rstd_, qn_, t1_, t2_ = lanes_t[lane]
                        ksq, krs, kqn, kt1, kt2 = ['a1_%s%d' % (n_, lane) for n_ in ('sq', 'rstd', 'qn', 't1', 't2')]
                        for k in range(8):
                            mm(bank(b_)[:, 0:ncols], w_in[:, k, col0:col0 + 128], uT[s2][:, k, 0:ncols], k == 0, k == 7,
                               r=['w_in', 'a1_uT%d' % s2], w=[pk(b_)])
                        yield
                        actf(sq_[:, 0:ncols], bank(b_)[:, 0:ncols], AF.Square, r=[pk(b_)], w=[ksq])
                        yield
                        mm(bank(ba)[:, 0:ncols], bones[:], sq_[:, 0:ncols], True, True, r=['bones', ksq], w=[pk(ba)])
                        yield
                        actf(rstd_[:, 0:ncols], bank(ba)[:, 0:ncols], AF.Sqrt, r=[pk(ba), 'epsc'], w=[krs], scale=1.0 / 64, bias=epsc[:, 0:1])
                        yield
                        (lambda ncols: P.dve(lambda e: e.reciprocal(out=rstd_[:, 0:ncols], in_=rstd_[:, 0:ncols]), r=[krs], w=[krs]))(ncols)
                        yield
                        if rope:
                            stt(qn_[:, 0:ncols], bank(b_)[:, 0:ncols], gcol[:, 0:1], rstd_[:, 0:ncols], ALU.mult, ALU.mult,
                                r=[pk(b_), k_g, krs], w=[kqn])
                            yield
                            mm(bank(ba)[:, 0:ncols], perm[:], qn_[:, 0:ncols], True, True, r=['perm', kqn], w=[pk(ba)])
                            tt('dve', t1_[:, 0:ncols], qn_[:, 0:ncols], cs[s2][:, 0:ncols], ALU.mult, r=[kqn, 'a1_cos%d' % s2], w=[kt1])
                            yield
                            tt('dve', t2_[:, 0:ncols], bank(ba)[:, 0:ncols], sn[s2][:, 0:ncols], ALU.mult, r=[pk(ba), 'a1_sin%d' % s2], w=[kt2])
                            tt('dve', dst, t1_[:, 0:ncols], t2_[:, 0:ncols], ALU.add, r=[kt1, kt2], w=[k_dst])
                            yield
                        else:
                            stt(dst, bank(b_)[:, 0:ncols], gcol[:, 0:1], rstd_[:, 0:ncols], ALU.mult, ALU.mult,
                                r=[pk(b_), k_g, krs], w=[k_dst])
                            yield

                    def proj_v(cj, ntiles, kt0):
                        s2 = cj % 2
                        for t_ in range(ntiles):
                            b_ = 4 + (t_ % 2)
                            for k in range(8):
                                mm(bank(b_)[:, 0:128], uT[s2][:, k, t_ * 128:(t_ + 1) * 128], w_in[:, k, 640:768], k == 0, k == 7,
                                   r=['w_in', 'a1_uT%d' % s2], w=[pk(b_)])
                            cp('act', VA[:, kt0 + t_, :, 0:64], bank(b_)[:, 0:128].rearrange("p (g d) -> p g d", g=2),
                               r=[pk(b_)], w=['VA'])

                    make_uT(ctx, 0, 2, modc, 'modc', 8)
                    make_uT(x, 0, 4, mod0, 'mod0', 1)
                    lockstep([proj_norm(8, 256, 512, gk, 'gk', False, KT[:, 0:256], 'KT', 0)])
                    proj_v(8, 2, 0)
                    for cj in range(8):
                        s2 = (cj + 1) % 2
                        if cj + 1 < 8:
                            make_uT(x, (cj + 1) * 512, 4, mod0, 'mod0', cj + 2)
                        dma_in(cs[s2][:], cosT[:, cj * 512:(cj + 1) * 512], ['a1_cos%d' % s2])
                        dma_in(sn[s2][:], sinT[:, cj * 512:(cj + 1) * 512], ['a1_sin%d' % s2])
                        cjp = cj + 1
                        for qp in range(2):
                            lockstep([proj_norm(cjp, 512, (qp * 2 + L) * 128, gq, 'gq', True,
                                                QT[:, qp * 2 + L, cj * 512:(cj + 1) * 512], 'QT', L) for L in range(2)])
                        lockstep([proj_norm(cjp, 512, 512, gk, 'gk', True, KT[:, LC + cj * 512: LC + (cj + 1) * 512], 'KT', 0)])
                        proj_v(cjp, 4, 2 + cj * 4)
                        for g_ in range(4):
                            b_ = (2, 3, 6, 7)[g_]
                            for k in range(8):
                                mm(bank(b_), w_in[:, k, 768 + g_ * 128: 768 + (g_ + 1) * 128], uT[s2][:, k, :], k == 0, k == 7,
                                   r=['w_in', 'a1_uT%d' % s2], w=[pk(b_)])
                            cp('act', pst[g_][:], bank(b_), r=[pk(b_)], w=['a1_pst%d' % g_])
                            dma_in(PT[g_, :, 8 + cj * 512: 8 + (cj + 1) * 512], pst[g_][:], ['PT_%d_%d' % (g_, cj)], ['a1_pst%d' % g_])
                if debug:
                    dma_in(dbg["d_qt"], QT[:].rearrange("p a b -> p (a b)"), ['d_qt'], ['QT'])
                    dma_in(dbg["d_kt"], KT[:], ['d_kt'], ['KT'])
                    dma_in(dbg["d_va"], VA[:].rearrange("p a b c -> p (a b c)"), ['d_va'], ['VA'])
                    final_keys += ['d_qt', 'd_kt', 'd_va', 'PT']

                if stop != 'A1':
                    with scope() as st:
                        w_oa = sb("w_oa", [64, 8, D], BF16, st)
                        w_op = sb("w_op", [128, 4, D], BF16, st)
                        w_gr = sb("w_gr", [128, 4, 128], BF16, st)
                        dma_in(w_oa[:], w_mix_out[0:512, :].rearrange("(h p) n -> p h n", p=64), ['w_oa'], q='pool')
                        dma_in(w_op[:], w_mix_out[512:1024, :].rearrange("(g p) n -> p g n", p=128), ['w_op'], q='pool')
                        dma_in(w_gr[:], w_pool_grp.rearrange("g c d -> c g d"), ['w_gr'], q='pool')
                        for hh in range(8):
                            tt('pool', w_oa[:, hh, :], w_oa[:, hh, :], mod0[0:64, 2, :], ALU.mult, r=['w_oa', 'mod0'], w=['w_oa'])
                        for g_ in range(4):
                            tt('pool', w_op[:, g_, :], w_op[:, g_, :], mod0[:, 2, :], ALU.mult, r=['w_op', 'mod0'], w=['w_op'])
                        pe_ = [sb("a2_pe%d" % i, [128, 1024], BF16, st) for i in range(3)]
                        rec = sb("a2_rec", [128, 512], F32, st)
                        aT = sb("a2_aT", [64, 8, 512], BF16, st)
                        plT = sb("a2_plT", [128, 4, 512], BF16, st)
                        ptl = [sb("a2_pt%d" % i, [128, 528], F32, st) for i in range(2)]
                        pa = sb("a2_pa", [128, 528], F32, st)
                        pb = sb("a2_pb", [128, 528], F32, st)
                        ic = sb("a2_ic", [128, 512], F32, st)
                        pld4 = [sb("a2_pld%d" % i, [128, 512], BF16, st) for i in range(4)]
                        wk = alloc_post_work(st)
                        ln0 = load_ln_rows(0, st, '0', 'm')
                        for cj in range(8):
                            q0 = cj * 512
                            for g_ in range(4):
                                w_ = WINS[g_]
                                s_ = g_ % 2
                                pt = ptl[s_]
                                dma_in(pt[:], PT[g_, :, q0:q0 + 528], ['a2_pt%d' % s_], ['PT'])
                                dma_in(ic[:], invcnt[g_:g_ + 1, q0:q0 + 512].broadcast_to([128, 512]), ['a2_ic'])
                                cur, kcur, ln_ = pt, 'a2_pt%d' % s_, 528
                                m_ = 1
                                bufs = [(pa, 'a2_pa'), (pb, 'a2_pb')]
                                bi = 0
                                while m_ < w_:
                                    nb, knb = bufs[bi]
                                    bi ^= 1
                                    nl = ln_ - m_
                                    tt('dve', nb[:, 0:nl], cur[:, 0:nl], cur[:, m_:m_ + nl], ALU.add, r=[kcur], w=[knb])
                                    cur, kcur, ln_ = nb, knb, nl
                                    m_ *= 2
                                o0 = 8 - w_ // 2
                                nb, knb = bufs[bi]
                                tt('dve', nb[:, 0:512], cur[:, o0:o0 + 512], ic[:], ALU.mult, r=[kcur, 'a2_ic'], w=[knb])
                                tt('dve', pld4[g_][:], nb[:, 0:512], pt[:, 8:520], ALU.subtract, r=[knb, 'a2_pt%d' % s_], w=['a2_pld%d' % g_])
                            LAG = 2
                            SB0 = (0, 2, 6)
                            for qc in range(4):
                                for step in range(NKT + LAG):
                                    if step < NKT:
                                        kt = step
                                        sl_ = kt % 3
                                        b0 = SB0[sl_]
                                        for hb in range(2):
                                            pr = slice(hb * 64, (hb + 1) * 64)
                                            mm(bank(b0 + hb), KT[pr, kt * 128:(kt + 1) * 128], QT[pr, qc, q0:q0 + 512], True, True,
                                               r=['KT', 'QT'], w=[pk(b0 + hb)])
                                        actf(pe_[sl_][:], bank(b0, 2), AF.Exp, r=[pk(b0), pk(b0 + 1)], w=['a2_pe%d' % sl_], scale=0.125)
                                    if step >= LAG:
                                        kt = step - LAG
                                        sl_ = kt % 3
                                        for hb in range(2):
                                            mm(bank(4 + hb), VA[:, kt, hb, :], pe_[sl_][:, hb * 512:(hb + 1) * 512], kt == 0, kt == NKT - 1,
                                               r=['VA', 'a2_pe%d' % sl_], w=[pk(4 + hb)])
                                for hb in range(2):
                                    head = qc + 4 * hb
                                    ob = bank(4 + hb)
                                    (lambda ob: P.dve(lambda e: e.reciprocal(out=rec[64:128, :], in_=ob[64:128, :]),
                                                      r=[pk(4 + hb)], w=['a2_rec']))(ob)
                                    tt('dve', aT[:, head, :], ob[0:64, :], rec[64:128, :], ALU.mult,
                                       r=[pk(4 + hb), 'a2_rec'], w=['a2_aT'])
                            for g_ in range(4):
                                mm(bank(0), w_gr[:, g_, :], pld4[g_][:], True, True, r=['w_gr', 'a2_pld%d' % g_], w=[pk(0)])
                                ts('dve', plT[:, g_, :], bank(0), pscale[:, g_:g_ + 1], None, ALU.mult, None,
                                   r=[pk(0), 'pscale'], w=['a2_plT'])
                            def y_emit_a2(t_):
                                def f(yb):
                                    for n in range(2):
                                        for hh in range(8):
                                            mm(bank(yb + n), aT[:, hh, t_ * 128:(t_ + 1) * 128], w_oa[:, hh, n * 512:(n + 1) * 512],
                                               hh == 0, False, r=['a2_aT', 'w_oa'], w=[pk(yb + n)])
                                        for g_ in range(4):
                                            mm(bank(yb + n), plT[:, g_, t_ * 128:(t_ + 1) * 128], w_op[:, g_, n * 512:(n + 1) * 512],
                                               False, g_ == 3, r=['a2_plT', 'w_op'], w=[pk(yb + n)])
                                return f
                            for tp in range(2):
                                gens = []
                                for L in range(2):
                                    t_ = tp * 2 + L
                                    ti = cj * 4 + t_
                                    gens.append(mixer_post_g(ti, x, y_emit_a2(t_), mod0, 'mod0', ln0, wk[L], wr0,
                                                             6 if L == 0 else 2, 4 if L == 0 else 0))
                                lockstep(gens)
            if stop not in ('A1', 'A2'):
                with scope() as st:
                    phase_route(0, st)
            if stop not in ('A1', 'A2', 'C0'):
                with scope() as st:
                    phase_experts(0, st)
                with scope() as st:
                    ln0f = load_ln_rows(0, st, '0', 'f')
                    phase_ffn_post(0, H, mod0, 'mod0', ln0f, H2, 'H2_', st)
        if debug:
            final_keys += ['d_y', 'd_afft', 'd_idx', 'd_gate', 'F'] + ['H%d' % t for t in range(NT)] + ['U2_%d' % t for t in range(NT)]

        if stop is None or stop in ('F', 'G', 'C1'):
            with scope() as L1:
                mod1 = sb("mod1", [128, 6, D], F32, L1)
                with scope() as st:
                    phase_mod(1, c_col, mod1, 12, 'mod1', st)
                wr1 = sb("wr1", [128, 8, NE], F32, L1)
                dma_in(wr1[:], w_router[1].rearrange("(k p) e -> p k e", p=128), ['wr'])
                with scope() as Fs:
                    uTa = sb("uTa", [128, 8, S], BF16, Fs)
                    with scope() as st:
                        ht = [sb("f0_h%d" % i, [128, D], F32, st) for i in range(2)]
                        ub = [sb("f0_ub%d" % i, [128, D], BF16, st) for i in range(2)]
                        for ti in range(NT):
                            s_ = ti % 2
                            dma_in(ht[s_][:], H2[ti * 128:(ti + 1) * 128, :], ['f0_h%d' % s_], ['H2_%d' % ti])
                            tt('pool', ht[s_][:], ht[s_][:], mod1[:, 1, :], ALU.mult, r=['f0_h%d' % s_, 'mod1'], w=['f0_h%d' % s_])
                            tt('dve', ub[s_][:], ht[s_][:], mod1[:, 0, :], ALU.add, r=['f0_h%d' % s_, 'mod1'], w=['f0_ub%d' % s_])
                            pT = bank(s_).bitcast(BF16)
                            for k in range(8):
                                tr(pT[:, k * 128:(k + 1) * 128], ub[s_][:, k * 128:(k + 1) * 128], ident_bf[:],
                                   r=['f0_ub%d' % s_, 'ident_bf'], w=[pk(s_)])
                            cp('act', uTa[:, :, ti * 128:(ti + 1) * 128], pT.rearrange("p (k t) -> p k t", k=8), r=[pk(s_)], w=['uTa'])
                    with scope() as st:
                        wci = [sb("f_wci%d" % i, [128, 8, 3, 128], BF16, st) for i in range(2)]
                        cx = sb("f_cx", [128, S + 2], F32, st)
                        z = sb("f_z", [128, S], F32, st)
                        csb = [sb("f_c%d" % i, [128, 512], F32, st) for i in range(2)]
                        bzc = [sb("f_bz%d" % i, [128, 512], BF16, st) for i in range(4)]
                        P.dve(lambda e: e.memset(cx[:, 0:1], 0.0), w=['f_cx'])
                        P.dve(lambda e: e.memset(cx[:, S + 1:S + 2], 0.0), w=['f_cx'])
                        wv = w_conv_in.rearrange("(k p) (j n) -> p k j n", p=128, j=3)
                        for i in range(8):
                            s_ = i % 2
                            for j3 in range(3):
                                dma_in(wci[s_][:, :, j3, :], wv[:, :, j3, i * 128:(i + 1) * 128], ['f_wci%d' % s_], q='pool')
                            for tc in range(8):
                                bc, bx = 2 + (tc % 2) * 2, 3 + (tc % 2) * 2
                                for k in range(8):
                                    mm(bank(bc), wci[s_][:, k, 1, :], uTa[:, k, tc * 512:(tc + 1) * 512], k == 0, k == 7,
                                       r=['f_wci%d' % s_, 'uTa'], w=[pk(bc)])
                                for k in range(8):
                                    mm(bank(bx), wci[s_][:, k, 2, :], uTa[:, k, tc * 512:(tc + 1) * 512], k == 0, k == 7,
                                       r=['f_wci%d' % s_, 'uTa'], w=[pk(bx)])
                                cp('act', csb[tc % 2][:], bank(bc), r=[pk(bc)], w=['f_c%d' % (tc % 2)])
                                tt('dve', cx[:, 1 + tc * 512: 1 + (tc + 1) * 512], csb[tc % 2][:], bank(bx), ALU.mult,
                                   r=['f_c%d' % (tc % 2), pk(bx)], w=['f_cx'])
                            HS = S // 2
                            BB = (6, 7, 0, 1)
                            for hz in range(2):
                                z0 = hz * HS
                                kz = 'f_z%d' % hz
                                ts('dve', z[:, z0:z0 + HS], cx[:, z0:z0 + HS], convw[:, i, 0:1], None, ALU.mult, None, r=['f_cx', 'convw'], w=[kz])
                                stt(z[:, z0:z0 + HS], cx[:, z0 + 1:z0 + HS + 1], convw[:, i, 1:2], z[:, z0:z0 + HS], ALU.mult, ALU.add,
                                    r=['f_cx', 'convw', kz], w=[kz])
                                stt(z[:, z0:z0 + HS], cx[:, z0 + 2:z0 + HS + 2], convw[:, i, 2:3], z[:, z0:z0 + HS], ALU.mult, ALU.add,
                                    r=['f_cx', 'convw', kz], w=[kz])
                                for tq in range(4):
                                    tc = hz * 4 + tq
                                    bb = BB[tq]
                                    for k in range(8):
                                        mm(bank(bb), wci[s_][:, k, 0, :], uTa[:, k, tc * 512:(tc + 1) * 512], k == 0, k == 7,
                                           r=['f_wci%d' % s_, 'uTa'], w=[pk(bb)])
                                    tt('dve', bzc[tq][:], bank(bb), z[:, tc * 512:(tc + 1) * 512], ALU.mult,
                                       r=[pk(bb), kz], w=['f_bz%d' % tq])
                                    dma_in(BZ[:, i, tc * 512:(tc + 1) * 512], bzc[tq][:], ['BZ_%d_%d' % (i, tc)], ['f_bz%d' % tq], q='pool')
                with scope() as st:
                    w_co = sb("w_co", [128, 8, D], BF16, st)
                    dma_in(w_co[:], w_conv_out.rearrange("(k p) n -> p k n", p=128), ['w_co'], q='pool')
                    for k in range(8):
                        tt('pool', w_co[:, k, :], w_co[:, k, :], mod1[:, 2, :], ALU.mult, r=['w_co', 'mod1'], w=['w_co'])
                    bzt = [sb("g_bz%d" % i, [128, 8, 128], BF16, st) for i in range(4)]
                    wk = alloc_post_work(st, 4)
                    ln1 = load_ln_rows(1, st, '1', 'm')
                    def y_emit_g(s_):
                        def f(yb):
                            for n in range(2):
                                for i in range(8):
                                    mm(bank(yb + n), bzt[s_][:, i, :], w_co[:, i, n * 512:(n + 1) * 512],
                                       i == 0, i == 7, r=['g_bz%d' % s_, 'w_co'], w=[pk(yb + n)])
                        return f
                    for W in wk:
                        W['res_keys'] = lambda t: ['H2_%d' % t]
                    for tp in range(NT // 4):
                        gens = []
                        for L in range(4):
                            ti = tp * 4 + L
                            dma_in(bzt[L][:], BZ[:, :, ti * 128:(ti + 1) * 128], ['g_bz%d' % L], ['BZ'])
                            gens.append(mixer_post_g(ti, H2, y_emit_g(L), mod1, 'mod1', ln1, wk[L], wr1, 2 * L, 2 * L))
                        lockstep(gens)
                if stop not in ('F', 'G'):
                    with scope() as st:
                        phase_route(1, st)
                if stop not in ('F', 'G', 'C1'):
                    with scope() as st:
                        phase_experts(1, st)
                    with scope() as st:
                        ln1f = load_ln_rows(1, st, '1', 'f')
                        phase_ffn_post(1, H, mod1, 'mod1', ln1f, out, 'out', st)
        P.emit(es, final_keys=final_keys)
    return nc, P


_CACHE = {}
CORE_SAMPLE = {0: 0, 1: 1, 2: 2, 3: 3}


def _consts():
    ident = np.eye(128, dtype=np.float32)
    bones = np.zeros((128, 128), np.float32)
    bones[0:64, 0:64] = 1.0
    bones[64:128, 64:128] = 1.0
    perm = np.zeros((128, 128), np.float32)
    for i in range(64):
        perm[2 * i + 1, 2 * i] = -1.0
        perm[2 * i, 2 * i + 1] = 1.0
    t = np.arange(S)
    row = (t // 64).astype(np.float32)
    col = (t % 64).astype(np.float32)
    inv_freq = (np.float32(10000.0) ** (-np.arange(0, 32, 2, dtype=np.float32) / np.float32(32))).astype(np.float32)
    ang = np.concatenate([row[:, None] * inv_freq[None, :], col[:, None] * inv_freq[None, :]], axis=-1).astype(np.float32)
    pair = (np.arange(128) % 64) // 2
    cosT = np.cos(ang).astype(np.float32).T[pair]
    sinT = np.sin(ang).astype(np.float32).T[pair]
    invcnt = np.zeros((4, S), np.float32)
    for g, w in enumerate(WINS):
        lo = np.maximum(t - w // 2, 0)
        hi = np.minimum(t + w // 2, S)
        invcnt[g] = (1.0 / (hi - lo).astype(np.float32)).astype(np.float32)
    bf = ml_dtypes.bfloat16
    return dict(ident_bf=ident.astype(bf), ident_f=ident, bones=bones.astype(bf), perm=perm.astype(bf),
                cosT=np.ascontiguousarray(cosT), sinT=np.ascontiguousarray(sinT), invcnt=invcnt)


def make_in_maps(inputs, ncores=8):
    f = lambda a: np.ascontiguousarray(np.asarray(a, dtype=np.float32))
    w_mix_in = f(inputs["w_mix_in"])[0]
    qcols = []
    for qc in range(4):
        qcols += list(range(qc * 64, qc * 64 + 64)) + list(range((4 + qc) * 64, (4 + qc) * 64 + 64))
    cols = qcols + list(range(512, 1280))
    w_mix_in_p = np.ascontiguousarray(w_mix_in[:, cols])
    shared = dict(
        w_mod=f(inputs["w_mod"]), b_mod=f(inputs["b_mod"]),
        ln_mix_g=f(inputs["ln_mix_g"]), ln_mix_b=f(inputs["ln_mix_b"]), ln_ffn_g=f(inputs["ln_ffn_g"]), ln_ffn_b=f(inputs["ln_ffn_b"]),
        w_mix_in=w_mix_in_p,
        gq_col=np.ascontiguousarray(np.tile(f(inputs["q_norm_g"])[0], 2).reshape(128, 1)),
        gk_col=np.ascontiguousarray(np.tile(f(inputs["k_norm_g"])[0], 2).reshape(128, 1)),
        w_pool_grp=f(inputs["w_pool_grp"])[0],
        pscale_col=np.ascontiguousarray(f(inputs["pool_scale"])[0].reshape(4, 128).T),
        w_mix_out=f(inputs["w_mix_out"])[0],
        w_conv_in=f(inputs["w_conv_in"])[0],
        convw_col=np.ascontiguousarray(f(inputs["conv_w"])[0].reshape(3, 8, 128).transpose(2, 1, 0)),
        w_conv_out=f(inputs["w_conv_out"])[0],
        w_router=f(inputs["w_router"]),
        w_exp_gate=f(inputs["w_exp_gate"]), w_exp_up=f(inputs["w_exp_up"]), w_exp_down=f(inputs["w_exp_down"]),
        cc_col=np.ascontiguousarray(f(inputs["c_ctx"]).reshape(8, 128).T),
    )
    shared.update(_consts())
    xs = f(inputs["x"]); cs = f(inputs["c"]); ctxs = f(inputs["ctx"])
    maps = []
    for i in range(ncores):
        b = i % 4
        m = dict(shared)
        m["x"] = xs[b]
        m["ctx"] = ctxs[b]
        m["c_col"] = np.ascontiguousarray(cs[b].reshape(8, 128).T)
        maps.append(m)
    return maps


def kernel(**inputs):
    if "nc" not in _CACHE:
        _CACHE["nc"] = build()[0]
    nc = _CACHE["nc"]
    maps = make_in_maps(inputs, 8)
    res = run_bass_kernel_spmd(nc, maps, core_ids=list(range(8)))
    core_of = {b: c for c, b in CORE_SAMPLE.items()}
    return np.stack([np.asarray(res.results[core_of[b]]["out"], dtype=np.float32) for b in range(4)], axis=0)
```
